# Optimizing a Trainium2 kernel written in Bass

```python
import math
import jax, jax.numpy as jnp
from jax import lax
import numpy as np

D_MODEL = 2048
BATCH = 4
SEQ = 4096
DEPTH = 2

CHUNK = 64
N_MIXERS = 2
EXPAND = 2
E_WIDTH = EXPAND * D_MODEL
EPS = 1e-6
M_HEAD_DIM = 64
M_HEADS = E_WIDTH // M_HEAD_DIM
M_GROUPS = 8
M_HEADS_PER_GROUP = M_HEADS // M_GROUPS
M_STATE = 128
M_CONV = 4
M_CONV_DIM = E_WIDTH + 2 * M_GROUPS * M_STATE
M_IN = 2 * E_WIDTH + 2 * M_GROUPS * M_STATE + M_HEADS
F_HEAD_DIM = 128
F_HEADS = E_WIDTH // F_HEAD_DIM
F_IN = 4 * E_WIDTH + F_HEADS
Q_BLOCK = 128
FORGET_BIAS_INIT = 5.0
N_MAMBA_LAYERS = (DEPTH + 1) // 2
N_FOX_LAYERS = DEPTH // 2

kernel_name = "hybrid_ssd_fox_streaming_encoder"


def rmsnorm(x, w):
    xf = x.astype(jnp.float32)
    y = xf * lax.rsqrt(jnp.mean(xf * xf, axis=-1, keepdims=True) + EPS)
    return (y * w.astype(jnp.float32)).astype(x.dtype)


def gated_group_rmsnorm(y, z, w):
    g = (y * jax.nn.silu(z)).astype(jnp.float32)
    shp = g.shape
    g = g.reshape(shp[:-1] + (M_GROUPS, shp[-1] // M_GROUPS))
    g = g * lax.rsqrt(jnp.mean(g * g, axis=-1, keepdims=True) + EPS)
    return (g.reshape(shp) * w.astype(jnp.float32)).astype(z.dtype)


def causal_depthwise_conv(u, w, bias):
    k = w.shape[0]
    out = lax.conv_general_dilated(
        u, w[:, None, :].astype(u.dtype), window_strides=(1,),
        padding=[(k - 1, 0)], dimension_numbers=('NWC', 'WIO', 'NWC'),
        feature_group_count=u.shape[-1])
    return out + bias.astype(u.dtype)


def ssd_chunked(xs, dt, a, bm, cm):
    xdt = xs * dt[..., None]
    acs = jnp.cumsum(dt * a, axis=2)
    causal = jnp.tril(jnp.ones((CHUNK, CHUNK), dtype=bool))
    seg = acs[:, :, :, None] - acs[:, :, None, :]
    lmat = jnp.exp(jnp.where(causal[:, :, None, None], seg, -jnp.inf))
    cb = jnp.einsum('bclgn,bcsgn->bclsg', cm, bm)
    y_diag = jnp.einsum('bclsgr,bcsgrp->bclgrp', cb[..., None] * lmat, xdt)
    decay_states = jnp.exp(acs[:, :, -1:] - acs)
    states = jnp.einsum('bclgn,bclgrp->bcgrpn', bm, xdt * decay_states[..., None])
    chunk_decay = jnp.exp(acs[:, :, -1])

    def step(carry, inp):
        st, dec = inp
        return carry * dec[..., None, None] + st, carry

    init = jnp.zeros_like(states[:, 0])
    _, prev = lax.scan(step, init, (jnp.moveaxis(states, 1, 0), jnp.moveaxis(chunk_decay, 1, 0)))
    prev = jnp.moveaxis(prev, 0, 1)
    y_off = jnp.einsum('bclgn,bcgrpn->bclgrp', cm, prev) * jnp.exp(acs)[..., None]
    return y_diag + y_off


def mamba2_mixer(h, w_in, conv_w, conv_b, dt_bias, a_log, d_skip, norm_w, w_out):
    b, s, _ = h.shape
    nc = s // CHUNK
    proj = h @ w_in
    z = proj[..., :E_WIDTH]
    xbc = proj[..., E_WIDTH:E_WIDTH + M_CONV_DIM]
    dt_raw = proj[..., E_WIDTH + M_CONV_DIM:]
    xbc = jax.nn.silu(causal_depthwise_conv(xbc, conv_w, conv_b))
    gn = M_GROUPS * M_STATE
    xs = xbc[..., :E_WIDTH].reshape(b, nc, CHUNK, M_GROUPS, M_HEADS_PER_GROUP, M_HEAD_DIM)
    bm = xbc[..., E_WIDTH:E_WIDTH + gn].reshape(b, nc, CHUNK, M_GROUPS, M_STATE)
    cm = xbc[..., E_WIDTH + gn:].reshape(b, nc, CHUNK, M_GROUPS, M_STATE)
    dt = jax.nn.softplus(dt_raw.astype(jnp.float32) + dt_bias.astype(jnp.float32))
    dt = dt.reshape(b, nc, CHUNK, M_GROUPS, M_HEADS_PER_GROUP)
    a = -jnp.exp(a_log.astype(jnp.float32)).reshape(M_GROUPS, M_HEADS_PER_GROUP)
    y = ssd_chunked(xs, dt, a, bm, cm)
    y = y + xs * d_skip.reshape(M_GROUPS, M_HEADS_PER_GROUP, 1)
    y = y.reshape(b, s, E_WIDTH).astype(h.dtype)
    y = gated_group_rmsnorm(y, z, norm_w)
    return y @ w_out


def forgetting_attention_mixer(h, w_in, b_forget, w_out):
    b, s, _ = h.shape
    proj = h @ w_in
    q = proj[..., :E_WIDTH].reshape(b, s, F_HEADS, F_HEAD_DIM)
    k = proj[..., E_WIDTH:2 * E_WIDTH].reshape(b, s, F_HEADS, F_HEAD_DIM)
    v = proj[..., 2 * E_WIDTH:3 * E_WIDTH].reshape(b, s, F_HEADS, F_HEAD_DIM)
    z = proj[..., 3 * E_WIDTH:4 * E_WIDTH]
    f_logit = proj[..., 4 * E_WIDTH:]
    log_f = jax.nn.log_sigmoid((f_logit + b_forget).astype(jnp.float32))
    cum = jnp.cumsum(log_f, axis=1).transpose(0, 2, 1)
    scale = 1.0 / math.sqrt(F_HEAD_DIM)
    outs = []
    for i in range(s // Q_BLOCK):
        q0 = i * Q_BLOCK
        q1 = q0 + Q_BLOCK
        logits = jnp.einsum('bqhd,bkhd->bhqk', q[:, q0:q1], k[:, :q1]).astype(jnp.float32) * scale
        logits = logits + cum[:, :, q0:q1, None] - cum[:, :, None, :q1]
        mask = (q0 + jnp.arange(Q_BLOCK))[:, None] >= jnp.arange(q1)[None, :]
        logits = jnp.where(mask, logits, -jnp.inf)
        p = jax.nn.softmax(logits, axis=-1).astype(v.dtype)
        outs.append(jnp.einsum('bhqk,bkhd->bqhd', p, v[:, :q1]))
    o = jnp.concatenate(outs, axis=1).reshape(b, s, E_WIDTH)
    o = o * jax.nn.silu(z)
    return o @ w_out


def setup_inputs(seed: int = 0) -> dict:
    key = jax.random.key(seed)
    ks = jax.random.split(key, 16)
    f32 = jnp.float32
    x = jax.random.normal(ks[0], (BATCH, SEQ, D_MODEL), f32)
    norm_w = 1.0 + 0.02 * jax.random.normal(ks[1], (DEPTH, D_MODEL), f32)
    nm = N_MAMBA_LAYERS
    m_w_in = jax.random.normal(ks[2], (nm, D_MODEL, M_IN), f32) * D_MODEL ** -0.5
    m_conv_w = jax.random.normal(ks[3], (nm, M_CONV, M_CONV_DIM), f32) * M_CONV ** -0.5
    m_conv_b = 0.02 * jax.random.normal(ks[4], (nm, M_CONV_DIM), f32)
    dt0 = jnp.exp(jax.random.uniform(ks[5], (nm, M_HEADS), f32, math.log(1e-3), math.log(1e-1)))
    m_dt_bias = dt0 + jnp.log(-jnp.expm1(-dt0))
    m_A_log = jnp.log(jax.random.uniform(ks[6], (nm, M_HEADS), f32, 1.0, 16.0))
    m_D = 1.0 + 0.02 * jax.random.normal(ks[7], (nm, M_HEADS), f32)
    m_norm_w = 1.0 + 0.02 * jax.random.normal(ks[8], (nm, E_WIDTH), f32)
    m_w_out = jax.random.normal(ks[9], (nm, E_WIDTH, D_MODEL), f32) * E_WIDTH ** -0.5
    nf = N_FOX_LAYERS
    f_w_in = jax.random.normal(ks[10], (nf, D_MODEL, F_IN), f32) * D_MODEL ** -0.5
    f_b_forget = FORGET_BIAS_INIT + 0.1 * jax.random.normal(ks[11], (nf, F_HEADS), f32)
    f_w_out = jax.random.normal(ks[12], (nf, E_WIDTH, D_MODEL), f32) * E_WIDTH ** -0.5
    final_norm_w = 1.0 + 0.02 * jax.random.normal(ks[13], (D_MODEL,), f32)
    return {"x": x, "norm_w": norm_w, "m_w_in": m_w_in, "m_conv_w": m_conv_w,
            "m_conv_b": m_conv_b, "m_dt_bias": m_dt_bias, "m_A_log": m_A_log,
            "m_D": m_D, "m_norm_w": m_norm_w, "m_w_out": m_w_out,
            "f_w_in": f_w_in, "f_b_forget": f_b_forget, "f_w_out": f_w_out,
            "final_norm_w": final_norm_w}


def reference(x, norm_w, m_w_in, m_conv_w, m_conv_b, m_dt_bias, m_A_log, m_D,
              m_norm_w, m_w_out, f_w_in, f_b_forget, f_w_out, final_norm_w):
    for i in range(DEPTH):
        h = rmsnorm(x, norm_w[i])
        j = i // N_MIXERS
        if i % N_MIXERS == 0:
            y = mamba2_mixer(h, m_w_in[j], m_conv_w[j], m_conv_b[j], m_dt_bias[j],
                             m_A_log[j], m_D[j], m_norm_w[j], m_w_out[j])
        else:
            y = forgetting_attention_mixer(h, f_w_in[j], f_b_forget[j], f_w_out[j])
        x = x + y.astype(x.dtype)
    return rmsnorm(x, final_norm_w)
```

```python
import math
from contextlib import ExitStack

import numpy as np
import concourse.bass as bass
import concourse.mybir as mybir
from concourse.bass_utils import run_bass_kernel_spmd

F32 = mybir.dt.float32
BF16 = mybir.dt.bfloat16
AF = mybir.ActivationFunctionType
ALU = mybir.AluOpType
AX = mybir.AxisListType

D_MODEL = 2048
KT = D_MODEL // 128
EPS = 1e-6
SEM_LIM = 30000


class Buf:
    __slots__ = ("name", "w", "r")

    def __init__(self, name=""):
        self.name = name
        self.w = {}
        self.r = {}


class Op:
    __slots__ = ("eng", "fn", "deps", "sig", "idx", "dma", "key", "val", "pos", "inc")


class Prog:
    ENG = ("pe", "act", "dve", "pool", "sp")

    def __init__(self, nc, stack):
        self.nc = nc
        self.stack = stack
        self.ops = {e: [] for e in self.ENG}
        self.dma_cnt = {}
        self.last_dma = {}

    @staticmethod
    def _k(op):
        return ("d", op.key) if op.dma else ("e", op.eng)

    def add(self, eng, fn, reads=(), writes=(), dma_key=None, inc=16, cwrites=()):
        op = Op()
        op.inc = inc
        op.eng = eng
        op.fn = fn
        op.sig = False
        op.idx = 0
        op.dma = dma_key is not None
        op.key = dma_key
        op.val = 0
        op.pos = len(self.ops[eng])
        deps = {}

        def need(d):
            if d is None or d is op:
                return
            if (not d.dma) and (not op.dma) and d.eng == "pe" and eng == "pe":
                return
            k = self._k(d)
            o = deps.get(k)
            if o is None or (d.val if d.dma else d.pos) > (o.val if o.dma else o.pos):
                deps[k] = d

        for b in reads:
            for d in b.w.values():
                need(d)
        for b in writes:
            for d in b.w.values():
                need(d)
            for d in b.r.values():
                need(d)
        op.deps = list(deps.values())
        for d in op.deps:
            d.sig = True
        if op.dma:
            c = self.dma_cnt.get(dma_key, 0) + inc
            self.dma_cnt[dma_key] = c
            op.val = c
        for b in reads:
            b.r[self._k(op)] = op
        for b in writes:
            b.w = {self._k(op): op}
            b.r = {}
        if op.dma:
            self.last_dma[dma_key] = op
        for b in cwrites:
            b.w[self._k(op)] = op
        self.ops[eng].append(op)
        return op

    def barrier(self):
        lasts = [self.ops[e][-1] for e in self.ENG if self.ops[e] and not self.ops[e][-1].dma]
        lasts = []
        for e in self.ENG:
            for op in reversed(self.ops[e]):
                if not op.dma and op.fn is not None:
                    lasts.append(op)
                    break
        dmas = [d for k, d in self.last_dma.items() if not (isinstance(k, tuple) and str(k[0]).startswith("ar"))]
        for e in self.ENG:
            op = Op()
            op.eng, op.fn, op.sig, op.idx, op.dma, op.key, op.val, op.inc = e, None, False, 0, False, None, 0, 0
            op.pos = len(self.ops[e])
            op.deps = [d for d in lasts if d.eng != e] + dmas
            for d in op.deps:
                d.sig = True
            self.ops[e].append(op)

    def emit(self):
        nc = self.nc
        st = self.stack
        self.esems = {}
        for e in self.ENG:
            c = 0
            for op in self.ops[e]:
                if (not op.dma) and op.sig:
                    c += 1
                    op.idx = c
            n = (c + SEM_LIM - 1) // SEM_LIM
            self.esems[e] = [st.enter_context(nc.semaphore(f"s_{e}{i}")) for i in range(max(n, 1))]
        self.dsems = {k: st.enter_context(nc.semaphore(f"d_{i}")) for i, k in enumerate(self.dma_cnt)}

        def sig_of(d):
            if d.dma:
                return self.dsems[d.key], d.val
            i = d.idx - 1
            return self.esems[d.eng][i // SEM_LIM], (i % SEM_LIM) + 1

        def run(ename, eng):
            waited = {}
            for op in self.ops[ename]:
                for d in op.deps:
                    s, v = sig_of(d)
                    if waited.get(id(s), 0) < v:
                        eng.wait_ge(s, v)
                        waited[id(s)] = v
                if op.fn is None:
                    continue
                ins = op.fn(eng)
                if op.dma:
                    ins.then_inc(self.dsems[op.key], op.inc)
                elif op.sig:
                    s, _ = sig_of(op)
                    ins.then_inc(s, 1)

        with nc.Block() as block:
            @block.tensor
            def _(eng):
                run("pe", eng)

            @block.scalar
            def _(eng):
                run("act", eng)

            @block.vector
            def _(eng):
                run("dve", eng)

            @block.gpsimd
            def _(eng):
                run("pool", eng)

            @block.sync
            def _(eng):
                run("sp", eng)


class Ctx:
    def __init__(self, nc, stack):
        self.nc = nc
        self.st = stack
        self.p = Prog(nc, stack)
        self.n = 0
        self.alloc_st = stack

    def scope(self):
        cx = self

        class _Scope:
            def __enter__(self_):
                self_.prev = cx.alloc_st
                self_.es = ExitStack()
                self_.es.__enter__()
                cx.alloc_st = self_.es
                return self_

            def __exit__(self_, *a):
                cx.p.barrier()
                cx.alloc_st = self_.prev
                return self_.es.__exit__(*a)

        return _Scope()

    def sb(self, shape, dt, name=None):
        self.n += 1
        t = self.alloc_st.enter_context(self.nc.sbuf_tensor(f"{name or 'sb'}_{self.n}", list(shape), dt))
        return t

    def ps(self, name=None):
        self.n += 1
        t = self.st.enter_context(self.nc.psum_tensor(f"{name or 'ps'}_{self.n}", [128, 512], F32))
        return t

    def dram(self, name, shape, dt, kind="Internal"):
        return self.nc.dram_tensor(name, list(shape), dt, kind=kind).ap()

    def dma(self, out, in_, reads, writes, key, q="sp", cwrites=()):
        return self.p.add(q, lambda e: e.dma_start(out=out, in_=in_), reads, writes, dma_key=key, cwrites=cwrites)

    def mm(self, out, lhsT, rhs, start, stop, reads, writes):
        return self.p.add("pe", lambda e: e.matmul(out, lhsT, rhs, start=start, stop=stop), reads, writes)

    def tr(self, out, in_, ident, reads, writes):
        return self.p.add("pe", lambda e: e.transpose(out, in_, ident), reads, writes)

    def act(self, out, in_, func, reads, writes, bias=None, scale=None, accum_out=None, eng="act", cw=()):
        kw = {}
        if bias is not None:
            kw["bias"] = bias
        if scale is not None:
            kw["scale"] = scale
        if accum_out is not None:
            kw["accum_out"] = accum_out
        return self.p.add(eng, lambda e: e.activation(out, in_, func, **kw), reads, writes, cwrites=cw)

    def tt(self, eng, out, in0, in1, op, reads, writes):
        return self.p.add(eng, lambda e: e.tensor_tensor(out, in0, in1, op), reads, writes)

    def ts(self, eng, out, in0, s1, s2, op0, op1, reads, writes):
        if op1 is None:
            return self.p.add(eng, lambda e: e.tensor_scalar(out, in0, s1, None, op0), reads, writes)
        return self.p.add(eng, lambda e: e.tensor_scalar(out, in0, s1, s2, op0, op1), reads, writes)

    def stt(self, eng, out, in0, scalar, in1, op0, op1, reads, writes):
        return self.p.add(eng, lambda e: e.scalar_tensor_tensor(out, in0, scalar, in1, op0, op1), reads, writes)

    def cp(self, eng, out, in_, reads, writes):
        if eng == "act":
            return self.p.add(eng, lambda e: e.copy(out, in_), reads, writes)
        return self.p.add(eng, lambda e: e.tensor_copy(out, in_), reads, writes)

    def memset(self, eng, ap, val, writes):
        return self.p.add(eng, lambda e: e.memset(ap, val), (), writes)


def host_consts():
    i = np.arange(128)
    ident = np.eye(128, dtype=np.float32)
    U = (i[:, None] <= i[None, :]).astype(np.float32)
    SM = (i[:, None] > i[None, :]).astype(np.float32)
    ones = np.ones((128, 128), np.float32)
    return np.ascontiguousarray(np.concatenate([ident, U, SM, ones], axis=1))


class Common:
    def __init__(self, cx, cst_ap):
        self.cx = cx
        nc = cx.nc
        self.cst = cx.sb([128, 512], F32, "cst")
        self.cst_b = Buf("cst")
        cx.dma(self.cst[:], cst_ap, (), [self.cst_b], "cst")
        self.ident_f = self.cst[:, 0:128]
        self.U = self.cst[:, 128:256]
        self.SM = self.cst[:, 256:384]
        self.ones = self.cst[:, 384:512]
        self.cstb = cx.sb([128, 512], BF16, "cstb")
        self.cstb_b = Buf("cstb")
        cx.cp("dve", self.cstb[:], self.cst[:], [self.cst_b], [self.cstb_b])
        self.ident_b = self.cstb[:, 0:128]
        self.U_b = self.cstb[:, 128:256]
        self.ones_b = self.cstb[:, 384:512]
        self.eps = cx.sb([128, 1], F32, "eps")
        self.eps_b = Buf("eps")
        cx.memset("dve", self.eps[:], EPS, [self.eps_b])
        self.banks = [cx.ps(f"bank{i}") for i in range(8)]
        self.bank_b = [Buf(f"bank{i}") for i in range(8)]
        self.wblk = [cx.sb([128, KT, 512], BF16, "wblk") for _ in range(2)]
        self.wblk_b = [[Buf(f"wblk{i}_{q}") for q in range(8)] for i in range(2)]
        self.wblk_i = 0
        self.pending = []

    def prefetch(self, W_ap, W_b, c0, ncols):
        if len(self.pending) >= 1:
            return
        self.pending.append(((id(W_b), c0, ncols), self.load_wblock(W_ap, W_b, c0, ncols)))

    def get(self, W_ap, W_b, c0, ncols):
        if self.pending:
            key, val = self.pending.pop(0)
            assert key == (id(W_b), c0, ncols), "weight prefetch order mismatch"
            return val
        return self.load_wblock(W_ap, W_b, c0, ncols)

    def load_wblock(self, W_ap, W_b, c0, ncols):
        cx = self.cx
        wi = self.wblk_i
        self.wblk_i ^= 1
        wt = self.wblk[wi]
        for q in range(8):
            src = W_ap[q * 256:(q + 1) * 256, c0:c0 + ncols].rearrange("(k p) c -> p k c", p=128)
            cx.dma(wt[:, q * 2:(q + 1) * 2, 0:ncols], src, [W_b], [self.wblk_b[wi][q]], ("w", wi, q), q="pool")
        return wt, self.wblk_b[wi]

    def load_small_w(self, dst, dst_b, W_ap, ncols, key):
        self.cx.dma(dst[:, :, 0:ncols], W_ap.rearrange("(k p) c -> p k c", p=128), (), [dst_b], key, q="pool")


class NormStage:
    def __init__(self, cx, cm, nwT_ap, name, nbuf=2):
        self.cx, self.cm = cx, cm
        self.nbuf = nbuf
        self.nw = cx.sb([128, KT], F32, "nwT")
        self.nw_b = Buf("nwT")
        cx.dma(self.nw[:], nwT_ap, (), [self.nw_b], ("nw", name))
        self.xt = [cx.sb([128, D_MODEL], F32, "xt") for _ in range(nbuf)]
        self.xt_b = [Buf("xt0"), Buf("xt1")]
        self.ss = [cx.sb([128, 1], F32, "ss") for _ in range(2)]
        self.ss_b = [Buf("ss0"), Buf("ss1")]
        self.rs = [cx.sb([128, 1], F32, "rs") for _ in range(2)]
        self.rs_b = [Buf("rs0"), Buf("rs1")]
        self.xn = [cx.sb([128, D_MODEL], BF16, "xn") for _ in range(nbuf)]
        self.xn_b = [Buf("xn0"), Buf("xn1")]
        self.i = 0
        self.xa = None

    def run(self, x_ap, x_b, row0, TT, hT, hT_b, bank, add_ap=None, add_b=None):
        cx, cm = self.cx, self.cm
        ps = cm.banks[bank]
        ps_b = cm.bank_b[bank]
        psv = ps[:].bitcast(BF16)
        for j in range(TT // 128):
            i = self.i
            self.i = (self.i + 1) % self.nbuf
            r0 = row0 + j * 128
            cx.dma(self.xt[i][:], x_ap[r0:r0 + 128, :], [x_b], [self.xt_b[i]], ("xt", i))
            if add_ap is not None:
                if self.xa is None:
                    self.xa = [cx.sb([128, D_MODEL], F32, "xa") for _ in range(2)]
                    self.xa_b = [Buf("xa0"), Buf("xa1")]
                ab = add_b[r0 // 512] if isinstance(add_b, list) else add_b
                cx.dma(self.xa[i][:], add_ap[r0:r0 + 128, :], [ab], [self.xa_b[i]], ("xa", i))
                cx.tt("dve", self.xt[i][:], self.xt[i][:], self.xa[i][:], ALU.add,
                      [self.xt_b[i], self.xa_b[i]], [self.xt_b[i]])
            cx.act(self.xn[i][:], self.xt[i][:], AF.Square, [self.xt_b[i]], [self.xn_b[i], self.ss_b[i]],
                   scale=1.0 / math.sqrt(D_MODEL), accum_out=self.ss[i][:])
            rstd(cx, self.rs[i], self.rs_b[i], self.ss[i], self.ss_b[i])
            cx.ts("dve", self.xn[i][:], self.xt[i][:], self.rs[i][:, 0:1], None, ALU.mult, None,
                  [self.xt_b[i], self.rs_b[i]], [self.xn_b[i]])
            transpose_scale(cx, cm, self.xn[i], self.xn_b[i], hT, hT_b, j * 128, self.nw, self.nw_b, bank)


def rstd(cx, out, out_b, ms, ms_b):
    cx.ts("dve", out[:], ms[:], EPS, None, ALU.add, None, [ms_b], [out_b])
    cx.act(out[:], out[:], AF.Sqrt, [out_b], [out_b])
    cx.p.add("dve", (lambda e, o=out: e.reciprocal(o[:], o[:])), [out_b], [out_b])


def sdev(cx, out, out_b, ms, ms_b, eps_ap, eps_b):
    cx.act(out[:], ms[:], AF.Sqrt, [ms_b, eps_b], [out_b], bias=eps_ap)


def transpose_scale(cx, cm, src, src_b, dstT, dstT_b, c0, wT, wT_b, bank):
    ps = cm.banks[bank]
    ps_b = cm.bank_b[bank]
    psv = ps[:].bitcast(BF16)
    for half in range(2):
        for k in range(8):
            kk = half * 8 + k
            cx.tr(psv[:, k * 128:(k + 1) * 128], src[:, kk * 128:(kk + 1) * 128], cm.ident_b,
                  [src_b, cm.cstb_b], [ps_b])
        cx.tt("dve", dstT[:, half * 8:(half + 1) * 8, c0:c0 + 128],
              psv.rearrange("p (k c) -> p k c", k=8),
              wT[:, half * 8:(half + 1) * 8].unsqueeze(2).to_broadcast([128, 8, 128]), ALU.mult,
              [ps_b, wT_b], [dstT_b])


def proj(cx, cm, hT, hT_b, TT, W_ap, W_b, blocks, banks, nxt=None):
    bi = 0
    for bidx, (c0, ncols, mode, consume) in enumerate(blocks):
        wt, wb = cm.get(W_ap, W_b, c0, ncols)
        if bidx + 1 < len(blocks):
            cm.prefetch(W_ap, W_b, blocks[bidx + 1][0], blocks[bidx + 1][1])
        elif nxt is not None:
            cm.prefetch(*nxt)
        if mode == "cm":
            for ct in range(ncols // 128):
                for t0 in range(0, TT, 512):
                    nt = min(512, TT - t0)
                    bank = banks[bi % len(banks)]
                    bi += 1
                    ps, psb = cm.banks[bank], cm.bank_b[bank]
                    for k in range(KT):
                        cx.mm(ps[:, 0:nt], wt[:, k, ct * 128:(ct + 1) * 128], hT[:, k, t0:t0 + nt],
                              k == 0, k == KT - 1, [wb[k // 2], hT_b], [psb])
                    consume(ps[:, 0:nt], psb, ct, t0, nt)
        else:
            for j in range(TT // 128):
                bank = banks[bi % len(banks)]
                bi += 1
                ps, psb = cm.banks[bank], cm.bank_b[bank]
                for k in range(KT):
                    cx.mm(ps[:, 0:ncols], hT[:, k, j * 128:(j + 1) * 128], wt[:, k, 0:ncols],
                          k == 0, k == KT - 1, [wb[k // 2], hT_b], [psb])
                consume(ps[:, 0:ncols], psb, j)


NH0 = 32
NG0 = 4
L0_COLS = 2048 + 512 + 512 + 2048


def l0_stage(cx, cm, S, x, x_b, nwT, W0, Wdt, cwv, cbv, hpv, mnwT, Wout, P, P_b, TT=512, dbg=None, on_chunk=None):
    if not isinstance(P_b, list):
        P_b = [P_b] * (S // 512)
    NCH = TT // 128
    ns = NormStage(cx, cm, nwT, "l0", nbuf=2)
    W0_b, Wout_b = Buf("W0"), Buf("Wout")
    hT = cx.sb([128, KT, TT], BF16, "hT")
    hT_b = Buf("hT")
    cw = cx.sb([128, 96], F32, "cw"); cw_b = Buf("cw")
    cx.dma(cw[:], cwv, (), [cw_b], "cw")
    cb = cx.sb([128, 24], F32, "cb"); cb_b = Buf("cb")
    cx.dma(cb[:], cbv, (), [cb_b], "cb")
    hp = cx.sb([128, 96], F32, "hp"); hp_b = Buf("hp")
    cx.dma(hp[:], hpv, (), [hp_b], "hp")
    dtb = hp[:, 0:32]
    Drep = hp[:, 64:96]
    Arep = cx.sb([128, 32], F32, "Arep"); A_b = Buf("Arep")
    cx.act(Arep[:], hp[:, 32:64], AF.Exp, [hp_b], [A_b])
    cx.ts("dve", Arep[:], Arep[:], -1.0, None, ALU.mult, None, [A_b], [A_b])
    mnw = cx.sb([128, KT], F32, "mnwT"); mnw_b = Buf("mnwT")
    cx.dma(mnw[:], mnwT, (), [mnw_b], "mnwT")
    wdt = cx.sb([128, KT, 32], BF16, "wdt"); wdt_b = Buf("wdt")
    cm.load_small_w(wdt, wdt_b, Wdt, 32, "wdt")
    pre = [cx.sb([128, 515], F32, "pre") for _ in range(2)]
    pre_b = [Buf("pre0"), Buf("pre1")]
    acc = [cx.sb([128, 512], F32, "acc") for _ in range(2)]
    acc_b = [Buf("acc0"), Buf("acc1")]
    xc = [cx.sb([128, 512], BF16, "xc") for _ in range(2)]
    xc_b = [Buf("xc0"), Buf("xc1")]
    halo = cx.sb([128, 24, 3], F32, "halo"); halo_b = Buf("halo")
    cx.memset("dve", halo[:], 0.0, [halo_b])
    xs_tok = cx.sb([128, NCH, 2048], BF16, "xs_tok"); xs_b = Buf("xs_tok")
    Btok = cx.sb([128, NCH, 512], BF16, "Btok"); Btok_b = Buf("Btok")
    BT = cx.sb([128, NG0, TT], BF16, "BT"); BT_b = Buf("BT")
    CT = cx.sb([128, NG0, TT], BF16, "CT"); CT_b = Buf("CT")
    zs = cx.sb([128, NCH, 2048], BF16, "zs"); zs_b = Buf("zs")
    dt = cx.sb([128, NCH, 32], F32, "dt"); dt_b = Buf("dt")
    dtA = cx.sb([128, NCH, 32], F32, "dtA"); dtA_b = Buf("dtA")
    St = cx.sb([128, 2048], F32, "S"); S_b = [Buf(f"S{g}") for g in range(NG0)]
    Sbf = cx.sb([128, 2048], BF16, "Sbf"); Sbf_b = [Buf(f"Sbf{g}") for g in range(NG0)]
    cx.memset("dve", St[:], 0.0, S_b)
    cx.memset("dve", Sbf[:], 0.0, Sbf_b)
    acs = cx.sb([128, 32], F32, "acs"); acs_b = Buf("acs")
    expacs = [cx.sb([128, 32], F32, "expacs") for _ in range(2)]; expacs_b = [Buf("expacs0"), Buf("expacs1")]
    cd = [cx.sb([128, 32], F32, "cd") for _ in range(2)]; cd_b = [Buf("cd0"), Buf("cd1")]
    dsts = [cx.sb([128, 32], F32, "dsts") for _ in range(2)]; dsts_b = [Buf("dsts0"), Buf("dsts1")]
    rhsm = [cx.sb([128, 4, 128], F32, "rhsm") for _ in range(2)]
    rhsm_b = [Buf("rhsm0"), Buf("rhsm1")]
    LT = [cx.sb([128, 4, 128], F32, "LT") for _ in range(2)]
    LT_b = [Buf("LT0"), Buf("LT1")]
    MT = [cx.sb([128, NH0, 128], BF16, "MT") for _ in range(2)]
    MT_b = [[Buf(f"MT{i}_{g}") for g in range(NG0)] for i in range(2)]
    cbm = cx.sb([128, NG0, 128], F32, "cbm"); cbm_b = Buf("cbm")
    xdt = cx.sb([128, 2048], BF16, "xdt"); xdt_b = Buf("xdt")
    xdtd = cx.sb([128, 2048], BF16, "xdtd"); xdtd_b = Buf("xdtd")
    t1 = [cx.sb([128, 512], F32, "t1") for _ in range(2)]
    t1_b = [Buf("t1_0"), Buf("t1_1")]
    t3 = [cx.sb([128, 512], F32, "t3") for _ in range(2)]
    t3_b = [Buf("t3_0"), Buf("t3_1")]
    gbuf = cx.sb([128, 2048], F32, "gbuf"); gbuf_b = [Buf(f"gbuf{g}") for g in range(NG0)]
    ssq = cx.sb([128, NG0], F32, "ssq"); ssq_b = Buf("ssq")
    rs4 = cx.sb([128, NG0], F32, "rs4"); rs4_b = Buf("rs4")
    ynorm = cx.sb([128, 2048], BF16, "ynorm"); ynorm_b = Buf("ynorm")
    ostage = [cx.sb([128, 512], F32, "ostage") for _ in range(2)]
    ostage_b = [Buf("ost0"), Buf("ost1")]
    cnt = {"conv": 0, "seg": 0, "o": 0}
    B2, B3, B4, B5, B6, B7 = 2, 3, 4, 5, 6, 7

    def bcast_mid(ap2, n):
        return ap2.unsqueeze(2).to_broadcast([128, ap2.shape[1], n])

    def bcast_outer(ap2, m):
        return ap2.unsqueeze(1).to_broadcast([128, m, ap2.shape[1]])

    for t in range(S // TT):
        row0 = t * TT
        ns.run(x, x_b, row0, TT, hT, hT_b, B2)

        deferred = []
        def conv_consume(base_ct, kind):
            def f(ps, psb, ct, t0, nt):
                ctg = base_ct + ct
                i = cnt["conv"] % 2
                cnt["conv"] += 1
                cx.cp("act", pre[i][:, 3:515], ps, [psb], [pre_b[i]])
                cx.cp("dve", pre[i][:, 0:3], halo[:, ctg, :], [halo_b], [pre_b[i]])
                cx.cp("dve", halo[:, ctg, :], pre[i][:, 512:515], [pre_b[i]], [halo_b])
                cx.ts("dve", acc[i][:], pre[i][:, 0:512], cw[:, ctg * 4:ctg * 4 + 1], None, ALU.mult, None,
                      [pre_b[i], cw_b], [acc_b[i]])
                for k in range(1, 4):
                    cx.stt("dve", acc[i][:], pre[i][:, k:k + 512], cw[:, ctg * 4 + k:ctg * 4 + k + 1], acc[i][:],
                           ALU.mult, ALU.add, [pre_b[i], cw_b, acc_b[i]], [acc_b[i]])
                if kind == "x":
                    dst, dst_b = xc[i][:], xc_b[i]
                elif kind == "B":
                    dst, dst_b = BT[:, ct, :], BT_b
                else:
                    dst, dst_b = CT[:, ct, :], CT_b
                cx.act(dst, acc[i][:], AF.Silu, [acc_b[i], cb_b], [dst_b], bias=cb[:, ctg:ctg + 1])
                if kind in ("x", "B"):
                    def do_tr(dst=dst, dst_b=dst_b, ctg=ctg, ct=ct, kind=kind):
                        psv = cm.banks[B2][:].bitcast(BF16)
                        for jj in range(NCH):
                            cx.tr(psv[:, jj * 128:(jj + 1) * 128], dst[:, jj * 128:(jj + 1) * 128], cm.ident_b,
                                  [dst_b, cm.cstb_b], [cm.bank_b[B2]])
                        src = psv[:, 0:NCH * 128].rearrange("p (j c) -> p j c", j=NCH)
                        if kind == "x":
                            cx.cp("dve", xs_tok[:, :, ctg * 128:(ctg + 1) * 128], src, [cm.bank_b[B2]], [xs_b])
                        else:
                            cx.cp("dve", Btok[:, :, ct * 128:(ct + 1) * 128], src, [cm.bank_b[B2]], [Btok_b])
                    while deferred:
                        deferred.pop(0)()
                    deferred.append(do_tr)
            return f

        def z_consume(zb):
            def f(ps, psb, j):
                while deferred:
                    deferred.pop(0)()
                cx.act(zs[:, j, zb * 512:(zb + 1) * 512], ps, AF.Silu, [psb], [zs_b])
            return f

        blocks = []
        for xb_ in range(4):
            blocks.append((xb_ * 512, 512, "cm", conv_consume(xb_ * 4, "x")))
        blocks.append((2048, 512, "cm", conv_consume(16, "B")))
        blocks.append((2560, 512, "cm", conv_consume(20, "C")))
        for zb in range(4):
            blocks.append((3072 + zb * 512, 512, "tm", z_consume(zb)))
        proj(cx, cm, hT, hT_b, TT, W0, W0_b, blocks[:6], [0, 1], nxt=(W0, W0_b, blocks[6][0], 512))
        b3, b3b = cm.banks[B3], cm.bank_b[B3]
        for j in range(NCH):
            for k in range(KT):
                cx.mm(b3[:, j * 32:(j + 1) * 32], hT[:, k, j * 128:(j + 1) * 128], wdt[:, k, :],
                      k == 0, k == KT - 1, [hT_b, wdt_b], [b3b])
        cx.tt("dve", dt[:], b3[:, 0:NCH * 32].rearrange("p (j r) -> p j r", j=NCH),
              bcast_outer(dtb, NCH), ALU.add, [b3b, hp_b], [dt_b])
        cx.act(dt[:], dt[:], AF.Exp, [dt_b], [dt_b])
        cx.act(dt[:], dt[:], AF.Ln, [dt_b], [dt_b], bias=1.0)
        cx.tt("dve", dtA[:], dt[:], bcast_outer(Arep[:], NCH), ALU.mult, [dt_b, A_b], [dtA_b])

        proj(cx, cm, hT, hT_b, TT, W0, W0_b, blocks[6:], [0, 1], nxt=(Wout, Wout_b, 0, 512))

        b3, b3b = cm.banks[B3], cm.bank_b[B3]
        b0, b0b = cm.banks[0], cm.bank_b[0]

        def stage_a(j):
            tok = slice(j * 128, (j + 1) * 128)
            pj = j % 2
            cx.mm(b3[:, 128:160], cm.U, dtA[:, j, :], True, True, [cm.cst_b, dtA_b], [b3b])
            cx.mm(b3[:, 160:192], cm.ones, dtA[:, j, :], True, True, [cm.cst_b, dtA_b], [b3b])
            cx.cp("dve", acs[:], b3[:, 128:160], [b3b], [acs_b])
            cx.act(expacs[pj][:], acs[:], AF.Exp, [acs_b], [expacs_b[pj]])
            cx.act(cd[pj][:], b3[:, 160:192], AF.Exp, [b3b], [cd_b[pj]])
            cx.tt("dve", dsts[pj][:], b3[:, 160:192], acs[:], ALU.subtract, [b3b, acs_b], [dsts_b[pj]])
            cx.act(dsts[pj][:], dsts[pj][:], AF.Exp, [dsts_b[pj]], [dsts_b[pj]])
            for g in range(NG0):
                cx.mm(b0[:, g * 128:(g + 1) * 128], BT[:, g, tok], CT[:, g, tok], True, True, [BT_b, CT_b], [b0b])
            cx.tt("dve", cbm[:], b0[:].rearrange("p (g l) -> p g l", g=NG0), bcast_outer(cm.U, NG0), ALU.mult,
                  [b0b, cm.cst_b], [cbm_b])
            for g in range(NG0):
                for hf in range(2):
                    r0 = g * 8 + hf * 4
                    i = cnt["seg"] % 2
                    cnt["seg"] += 1
                    sbank = (B4, 1)[i]
                    cx.tt("pool", rhsm[i][:], bcast_outer(cm.U, 4), bcast_mid(dtA[:, j, r0:r0 + 4], 128), ALU.mult,
                          [cm.cst_b, dtA_b], [rhsm_b[i]])
                    cx.mm(cm.banks[sbank][:], cm.SM, rhsm[i][:].rearrange("p r l -> p (r l)"), True, True,
                          [cm.cst_b, rhsm_b[i]], [cm.bank_b[sbank]])
                    cx.act(LT[i][:].rearrange("p r l -> p (r l)"), cm.banks[sbank][:], AF.Exp,
                           [cm.bank_b[sbank]], [LT_b[i]])
                    cx.tt("dve", MT[pj][:, r0:r0 + 4, :], LT[i][:], bcast_outer(cbm[:, g, :], 4), ALU.mult,
                          [LT_b[i], cbm_b], [MT_b[pj][g]])

        def stage_b(j):
            tok = slice(j * 128, (j + 1) * 128)
            pj = j % 2
            cx.tt("pool", xdt[:].rearrange("p (r c) -> p r c", r=NH0),
                  xs_tok[:, j, :].rearrange("p (r c) -> p r c", r=NH0), bcast_mid(dt[:, j, :], 64), ALU.mult,
                  [xs_b, dt_b], [xdt_b])
            cx.tt("pool", xdtd[:].rearrange("p (r c) -> p r c", r=NH0),
                  xdt[:].rearrange("p (r c) -> p r c", r=NH0), bcast_mid(dsts[pj][:], 64), ALU.mult,
                  [xdt_b, dsts_b[pj]], [xdtd_b])
            for g in range(NG0):
                gs = slice(g * 512, (g + 1) * 512)
                b5, b5b = cm.banks[B5], cm.bank_b[B5]
                for r in range(8):
                    hh_ = g * 8 + r
                    cx.mm(b5[:, r * 64:(r + 1) * 64], MT[pj][:, hh_, :], xdt[:, hh_ * 64:(hh_ + 1) * 64], True, True,
                          [MT_b[pj][g], xdt_b], [b5b])
                b6, b6b = cm.banks[B6], cm.bank_b[B6]
                cx.mm(b6[:], CT[:, g, tok], Sbf[:, gs], True, True, [CT_b, Sbf_b[g]], [b6b])
                b7, b7b = cm.banks[B7], cm.bank_b[B7]
                cx.mm(b7[:], Btok[:, j, g * 128:(g + 1) * 128], xdtd[:, gs], True, True, [Btok_b, xdtd_b], [b7b])
                ti = g % 2
                cx.tt("dve", t1[ti][:].rearrange("p (r c) -> p r c", r=8), b6[:].rearrange("p (r c) -> p r c", r=8),
                      bcast_mid(expacs[pj][:, g * 8:(g + 1) * 8], 64), ALU.mult, [b6b, expacs_b[pj]], [t1_b[ti]])
                cx.tt("dve", t1[ti][:], t1[ti][:], b5[:], ALU.add, [t1_b[ti], b5b], [t1_b[ti]])
                cx.tt("pool", t3[ti][:].rearrange("p (r c) -> p r c", r=8),
                      xs_tok[:, j, gs].rearrange("p (r c) -> p r c", r=8),
                      bcast_mid(Drep[:, g * 8:(g + 1) * 8], 64), ALU.mult, [xs_b, hp_b], [t3_b[ti]])
                cx.tt("dve", t1[ti][:], t1[ti][:], t3[ti][:], ALU.add, [t1_b[ti], t3_b[ti]], [t1_b[ti]])
                if dbg is not None and "y" in dbg:
                    cx.dma(dbg["y"][row0 + j * 128:row0 + (j + 1) * 128, gs], t1[ti][:], [t1_b[ti]], [dbg["y_b"]],
                           ("dbgy", ti))
                cx.tt("dve", gbuf[:, gs], t1[ti][:], zs[:, j, gs], ALU.mult, [t1_b[ti], zs_b], [gbuf_b[g]])
                cx.act(t3[ti][:], gbuf[:, gs], AF.Square, [gbuf_b[g]], [t3_b[ti], ssq_b],
                       scale=1.0 / math.sqrt(512.0), accum_out=ssq[:, g:g + 1])
                cx.tt("dve", St[:, gs].rearrange("p (r c) -> p r c", r=8), St[:, gs].rearrange("p (r c) -> p r c", r=8),
                      bcast_mid(cd[pj][:, g * 8:(g + 1) * 8], 64), ALU.mult, [S_b[g], cd_b[pj]], [S_b[g]])
                cx.tt("dve", St[:, gs], St[:, gs], b7[:], ALU.add, [S_b[g], b7b], [S_b[g]])
                cx.cp("act", Sbf[:, gs], St[:, gs], [S_b[g]], [Sbf_b[g]])
            rstd(cx, rs4, rs4_b, ssq, ssq_b)
            for g in range(NG0):
                cx.act(ynorm[:, g * 512:(g + 1) * 512], gbuf[:, g * 512:(g + 1) * 512], AF.Copy, [gbuf_b[g], rs4_b],
                       [ynorm_b] if g == 0 else (), scale=rs4[:, g:g + 1], **({} if g == 0 else {"cw": [ynorm_b]}))
            transpose_scale(cx, cm, ynorm, ynorm_b, hT, hT_b, j * 128, mnw, mnw_b, B2)

        stage_a(0)
        for j in range(NCH):
            if j + 1 < NCH:
                stage_a(j + 1)
            stage_b(j)

        def o_consume(c0):
            def f(ps, psb, j):
                i = cnt["o"] % 2
                cnt["o"] += 1
                cx.cp("act", ostage[i][:], ps, [psb], [ostage_b[i]])
                cx.dma(P[row0 + j * 128:row0 + (j + 1) * 128, c0:c0 + 512], ostage[i][:], [ostage_b[i]], (),
                       ("ost", i), cwrites=[P_b[(row0 + j * 128) // 512]])
            return f

        proj(cx, cm, hT, hT_b, TT, Wout, Wout_b, [(c0, 512, "tm", o_consume(c0)) for c0 in range(0, 2048, 512)], [0, 1],
             nxt=(W0, W0_b, 0, 512) if t + 1 < S // TT else None)
        if on_chunk is not None:
            on_chunk(list(range(row0 // 512, (row0 + TT) // 512)))


def l0_host_inputs(inp, b, hh):
    E = 4096
    w_in = inp["m_w_in"][0]
    xs_ = slice(E + hh * 2048, E + (hh + 1) * 2048)
    Bs = slice(2 * E + hh * 512, 2 * E + (hh + 1) * 512)
    Cs = slice(2 * E + 1024 + hh * 512, 2 * E + 1024 + (hh + 1) * 512)
    zs_ = slice(hh * 2048, (hh + 1) * 2048)
    dts = slice(2 * E + 2048 + hh * 32, 2 * E + 2048 + (hh + 1) * 32)
    W0 = np.ascontiguousarray(np.concatenate([w_in[:, xs_], w_in[:, Bs], w_in[:, Cs], w_in[:, zs_]], axis=1))
    Wdt = np.ascontiguousarray(w_in[:, dts])
    cidx = np.concatenate([np.arange(hh * 2048, (hh + 1) * 2048), 4096 + np.arange(hh * 512, (hh + 1) * 512),
                           5120 + np.arange(hh * 512, (hh + 1) * 512)])
    cwv = inp["m_conv_w"][0][:, cidx]
    cwv = np.ascontiguousarray(cwv.reshape(4, 24, 128).transpose(2, 1, 0).reshape(128, 96))
    cbv = np.ascontiguousarray(inp["m_conv_b"][0][cidx].reshape(24, 128).T)
    hs = slice(hh * 32, (hh + 1) * 32)
    hp = np.concatenate([inp["m_dt_bias"][0][hs], inp["m_A_log"][0][hs], inp["m_D"][0][hs]])
    hp = np.ascontiguousarray(np.broadcast_to(hp, (128, 96)))
    mnwT = np.ascontiguousarray(inp["m_norm_w"][0][hh * 2048:(hh + 1) * 2048].reshape(16, 128).T)
    Wout = np.ascontiguousarray(inp["m_w_out"][0][hh * 2048:(hh + 1) * 2048, :])
    nwT = np.ascontiguousarray(inp["norm_w"][0].reshape(16, 128).T)
    return {"nwT": nwT, "W0": W0, "Wdt": Wdt, "cwv": cwv, "cbv": cbv, "hpv": hp, "mnwT": mnwT, "Wout": Wout,
            "cst": host_consts()}


def build_l0(S, TT=512, dbg=False):
    nc = bass.Bass("TRN2", target_bir_lowering=False)
    with ExitStack() as st:
        cx = Ctx(nc, st)
        x = cx.dram("x", [S, D_MODEL], F32, "ExternalInput")
        cst = cx.dram("cst", [128, 512], F32, "ExternalInput")
        nwT = cx.dram("nwT", [128, KT], F32, "ExternalInput")
        W0 = cx.dram("W0", [D_MODEL, L0_COLS], F32, "ExternalInput")
        Wdt = cx.dram("Wdt", [D_MODEL, 32], F32, "ExternalInput")
        cwv = cx.dram("cwv", [128, 96], F32, "ExternalInput")
        cbv = cx.dram("cbv", [128, 24], F32, "ExternalInput")
        hpv = cx.dram("hpv", [128, 96], F32, "ExternalInput")
        mnwT = cx.dram("mnwT", [128, KT], F32, "ExternalInput")
        Wout = cx.dram("Wout", [D_MODEL, D_MODEL], F32, "ExternalInput")
        P = cx.dram("P", [S, D_MODEL], F32, "ExternalOutput")
        P_b = Buf("P")
        d = None
        if dbg:
            d = {"y": cx.dram("dbg_y", [S, 2048], F32, "ExternalOutput"), "y_b": Buf("dbg_y")}
        cm = Common(cx, cst)
        l0_stage(cx, cm, S, x, Buf("x"), nwT, W0, Wdt, cwv, cbv, hpv, mnwT, Wout, P, P_b, TT=TT, dbg=d)
        fin = [P_b] + ([d["y_b"]] if d else [])
        cx.p.add("sp", None, fin, ())
        cx.p.emit()
    return nc


NH1 = 16
L1_COLS = 4 * 2048


def l1_stage(cx, cm, S, x, x_b, nwT, W1, Wf, bfv, Wout, P, P_b, scr, TT=2048, add_ap=None, add_b=None,
             on_tile=None, p3_setup=None, mid_hook=None):
    if not isinstance(P_b, list):
        P_b = [P_b] * (S // 512)
    TT = min(TT, S)
    NT = S // 128
    NQB = S // 512
    QT_, KT_, ZT_, V_, OG_ = scr["QT"], scr["KT"], scr["ZT"], scr["V"], scr["OG"]
    QT_b, KT_b, ZT_b, V_b, OG_b = Buf("QT"), Buf("KT"), Buf("ZT"), Buf("V"), Buf("OG")
    W1_b, Wout_b = Buf("W1"), Buf("Wout1")
    bfr = cx.sb([128, NH1], F32, "bfr"); bfr_b = Buf("bfr")
    cx.dma(bfr[:], bfv, (), [bfr_b], "bfr")
    fl = cx.sb([128, NT, NH1], F32, "fl"); fl_b = Buf("fl")
    cum = cx.sb([128, NT, NH1], F32, "cum"); cum_b = Buf("cum")
    cref = cx.sb([128, NQB, NH1], F32, "cref"); cref_b = Buf("cref")
    carry = [cx.sb([128, NH1], F32, "carry") for _ in range(2)]
    carry_b = [Buf("carry0"), Buf("carry1")]
    cnt = {"st": 0, "o": 0, "pt": 0, "sb": 0}

    with cx.scope():
        ns = NormStage(cx, cm, nwT, "l1")
        hT = cx.sb([128, KT, TT], BF16, "hT1")
        hT_b = Buf("hT1")
        wf = cx.sb([128, KT, NH1], BF16, "wf"); wf_b = Buf("wf")
        cm.load_small_w(wf, wf_b, Wf, NH1, "wf")
        NST = 4
        st = [cx.sb([128, 512], BF16, "st1") for _ in range(NST)]
        st_b = [Buf(f"st1_{i}") for i in range(NST)]
        cm.prefetch(W1, W1_b, 0, 512)
        for t in range(S // TT):
            row0 = t * TT
            ns.run(x, x_b, row0, TT, hT, hT_b, 2, add_ap=add_ap, add_b=add_b)

            def cm_consume(dst, dst_b, blk, silu):
                def f(ps, psb, ct, t0, nt):
                    h = blk * 4 + ct
                    i = cnt["st"] % NST
                    cnt["st"] += 1
                    if silu:
                        cx.act(st[i][:, 0:nt], ps, AF.Silu, [psb], [st_b[i]])
                    else:
                        cx.cp(("act", "dve")[i % 2], st[i][:, 0:nt], ps, [psb], [st_b[i]])
                    cx.dma(dst[h, :, row0 + t0:row0 + t0 + nt], st[i][:, 0:nt], [st_b[i]], (), ("st1", i),
                           cwrites=[dst_b])
                return f

            def v_consume(vb):
                def f(ps, psb, j):
                    i = cnt["st"] % NST
                    cnt["st"] += 1
                    cx.cp(("act", "dve")[i % 2], st[i][:], ps, [psb], [st_b[i]])
                    cx.dma(V_[row0 + j * 128:row0 + (j + 1) * 128, vb * 512:(vb + 1) * 512], st[i][:], [st_b[i]], (),
                           ("st1", i), cwrites=[V_b])
                return f

            blocks = []
            for blk in range(4):
                blocks.append((blk * 512, 512, "cm", cm_consume(QT_, QT_b, blk, False)))
            for blk in range(4):
                blocks.append((2048 + blk * 512, 512, "cm", cm_consume(KT_, KT_b, blk, False)))
            for blk in range(4):
                blocks.append((4096 + blk * 512, 512, "cm", cm_consume(ZT_, ZT_b, blk, True)))
            for blk in range(4):
                blocks.append((6144 + blk * 512, 512, "tm", v_consume(blk)))
            proj(cx, cm, hT, hT_b, TT, W1, W1_b, blocks, [0, 1],
                 nxt=(W1, W1_b, 0, 512) if t + 1 < S // TT else None)
            b3, b3b = cm.banks[3], cm.bank_b[3]
            ntl = TT // 128
            for j in range(ntl):
                for k in range(KT):
                    cx.mm(b3[:, j * NH1:(j + 1) * NH1], hT[:, k, j * 128:(j + 1) * 128], wf[:, k, :],
                          k == 0, k == KT - 1, [hT_b, wf_b], [b3b])
            cx.tt("dve", fl[:, t * ntl:(t + 1) * ntl, :], b3[:, 0:ntl * NH1].rearrange("p (j r) -> p j r", j=ntl),
                  bfr[:].unsqueeze(1).to_broadcast([128, ntl, NH1]), ALU.add, [b3b, bfr_b], [fl_b])

        cx.act(fl[:], fl[:], AF.Exp, [fl_b], [fl_b], scale=-1.0)
        cx.act(fl[:], fl[:], AF.Ln, [fl_b], [fl_b], bias=1.0)
        cx.ts("dve", fl[:], fl[:], -1.0, None, ALU.mult, None, [fl_b], [fl_b])
        cx.memset("dve", carry[0][:], 0.0, [carry_b[0]])
        b3, b3b = cm.banks[3], cm.bank_b[3]
        for i in range(NT):
            c0, c1 = carry[i % 2], carry[(i + 1) % 2]
            c0b, c1b = carry_b[i % 2], carry_b[(i + 1) % 2]
            cx.mm(b3[:, 0:NH1], cm.U, fl[:, i, :], True, True, [cm.cst_b, fl_b], [b3b])
            cx.mm(b3[:, NH1:2 * NH1], cm.ones, fl[:, i, :], True, True, [cm.cst_b, fl_b], [b3b])
            cx.tt("dve", cum[:, i, :], b3[:, 0:NH1], c0[:], ALU.add, [b3b, c0b], [cum_b])
            if i % 4 == 0:
                cx.cp("dve", cref[:, i // 4, :], c0[:], [c0b], [cref_b])
            cx.tt("dve", c1[:], b3[:, NH1:2 * NH1], c0[:], ALU.add, [b3b, c0b], [c1b])

    with cx.scope():
        Kh = [cx.sb([128, S], BF16, "Kh") for _ in range(2)]
        Kh_b = [Buf("Kh0"), Buf("Kh1")]
        Vh = [cx.sb([128, NT, 128], BF16, "Vh") for _ in range(2)]
        Vh_b = [Buf("Vh0"), Buf("Vh1")]
        Qs = [cx.sb([128, 512], BF16, "Qs") for _ in range(3)]
        Qs_b = [Buf(f"Qs{i}") for i in range(3)]
        Zs = [cx.sb([128, 512], BF16, "Zs") for _ in range(3)]
        Zs_b = [Buf(f"Zs{i}") for i in range(3)]
        NPT = 6
        PT = [cx.sb([128, 512], BF16, "PT") for _ in range(NPT)]
        PT_b = [Buf(f"PT{i}") for i in range(NPT)]
        bb = [cx.sb([128, NT], F32, "bb") for _ in range(3)]
        bb_b = [Buf(f"bb{i}") for i in range(3)]
        rden = [cx.sb([128, 512], F32, "rden") for _ in range(2)]
        rden_b = [Buf("rden0"), Buf("rden1")]
        og = [cx.sb([128, 512], F32, "og") for _ in range(2)]
        og_b = [Buf("og0"), Buf("og1")]
        ogz = [cx.sb([128, 512], BF16, "ogz") for _ in range(2)]
        ogz_b = [Buf("ogz0"), Buf("ogz1")]
        TT3 = min(1024, S)
        hT3 = cx.sb([128, KT, TT3], BF16, "hT3")
        hT3_b = Buf("hT3")
        ostage = [cx.sb([128, 512], F32, "ostage1") for _ in range(2)]
        ostage_b = [Buf("ost1_0"), Buf("ost1_1")]
        if p3_setup is not None:
            p3_setup()
        sc = 1.0 / math.sqrt(128.0)
        gctr = {"g": 0, "idx": 0}

        def attention_pass(qlo, qhi, mid_hook=None):
            ntok = qhi * 512
            nq = qhi - qlo

            def load_head(h):
                hi = h % 2
                cx.dma(Kh[hi][:, 0:ntok], KT_[h, :, 0:ntok], [KT_b], [Kh_b[hi]], ("Kh", hi))
                cx.dma(Vh[hi][:, 0:ntok // 128, :],
                       V_[0:ntok, h * 128:(h + 1) * 128].rearrange("(t p) d -> p t d", p=128), [V_b], [Vh_b[hi]],
                       ("Vh", hi))

            g0 = gctr["g"]

            def load_q(gl):
                h, qo = divmod(gl, nq)
                qb = qlo + qo
                i3 = (g0 + gl) % 3
                cx.dma(Qs[i3][:], QT_[h, :, qb * 512:(qb + 1) * 512], [QT_b], [Qs_b[i3]], ("Qs", i3))
                cx.dma(Zs[i3][:], ZT_[h, :, qb * 512:(qb + 1) * 512], [ZT_b], [Zs_b[i3]], ("Zs", i3))
                nk = 4 * qb + 4
                cx.ts("dve", bb[i3][:, 0:nk], cum[:, 0:nk, h], cref[:, qb, h:h + 1], -1.0, ALU.subtract, ALU.mult,
                      [cum_b, cref_b], [bb_b[i3]])

            items = []
            for h in range(NH1):
                for qo in range(nq):
                    for kt in range(4 * (qlo + qo) + 4):
                        items.append((h, qo, kt))
            LA = 2
            load_head(0)
            load_q(0)
            ib = gctr["idx"]

            def issue_qk(li):
                h, qo, kt = items[li]
                qb = qlo + qo
                gl = h * nq + qo
                i3 = (g0 + gl) % 3
                hi = h % 2
                if kt == 0 and gl + 1 < NH1 * nq:
                    load_q(gl + 1)
                if kt == 3 and qo == 0 and h + 1 < NH1:
                    load_head(h + 1)
                if False and mid_hook is not None:
                    mid_hook(h == NH1 - 1)
                m = kt - 4 * qb
                q0 = max(m, 0) * 128
                si = (ib + li) % 3
                sbk, sbb = cm.banks[si], cm.bank_b[si]
                cx.mm(sbk[:, q0:512], Kh[hi][:, kt * 128:(kt + 1) * 128], Qs[i3][:, q0:512], True, True,
                      [Kh_b[hi], Qs_b[i3]], [sbb])

            def issue_rest(li):
                h, qo, kt = items[li]
                qb = qlo + qo
                gl = h * nq + qo
                i2 = (g0 + gl) % 2
                i3 = (g0 + gl) % 3
                hi = h % 2
                nk = 4 * qb + 4
                m = kt - 4 * qb
                q0 = max(m, 0) * 128
                si = (ib + li) % 3
                sbk, sbb = cm.banks[si], cm.bank_b[si]
                pi = (ib + li) % NPT
                ob, obb = cm.banks[4 + i2], cm.bank_b[4 + i2]
                db, dbb = cm.banks[6 + i2], cm.bank_b[6 + i2]
                cx.act(PT[pi][:, q0:512], sbk[:, q0:512], AF.Exp, [sbb, bb_b[i3]], [PT_b[pi]],
                       bias=bb[i3][:, kt:kt + 1], scale=sc)
                if m >= 0:
                    cx.tt("dve", PT[pi][:, q0:q0 + 128], PT[pi][:, q0:q0 + 128], cm.U_b, ALU.mult,
                          [PT_b[pi], cm.cstb_b], [PT_b[pi]])
                cx.mm(ob[:, q0:512], Vh[hi][:, kt, :], PT[pi][:, q0:512], kt == 0, kt == nk - 1,
                      [Vh_b[hi], PT_b[pi]], [obb])
                cx.mm(db[:, q0:512], cm.ones_b, PT[pi][:, q0:512], kt == 0, kt == nk - 1,
                      [cm.cstb_b, PT_b[pi]], [dbb])
                if kt == nk - 1:
                    cx.p.add("dve", (lambda e, o=rden[i2], d=db: e.reciprocal(o[:], d[:])), [dbb], [rden_b[i2]])
                    cx.tt("dve", og[i2][:], ob[:], rden[i2][:], ALU.mult, [obb, rden_b[i2]], [og_b[i2]])
                    cx.tt("dve", ogz[i2][:], og[i2][:], Zs[i3][:], ALU.mult, [og_b[i2], Zs_b[i3]], [ogz_b[i2]])
                    cx.dma(OG_[h, :, qb * 512:(qb + 1) * 512], ogz[i2][:], [ogz_b[i2]], (), ("ogz", i2),
                           cwrites=[OG_b])

            for li in range(len(items) + LA):
                if li < len(items):
                    issue_qk(li)
                if li >= LA:
                    issue_rest(li - LA)
            gctr["g"] += NH1 * nq
            gctr["idx"] += len(items)

        def phase3(row_lo, row_hi, is_last):
            ntile = (row_hi - row_lo) // TT3
            cm.prefetch(Wout, Wout_b, 0, 512)
            for t in range(ntile):
                row0 = row_lo + t * TT3
                cx.dma(hT3[:, :, 0:TT3], OG_[:, :, row0:row0 + TT3].rearrange("c p t -> p c t"), [OG_b], [hT3_b],
                       "hT1ld")

                def o_consume(c0, row0=row0):
                    def f(ps, psb, j):
                        i = cnt["o"] % 2
                        cnt["o"] += 1
                        cx.cp(("act", "dve")[i], ostage[i][:], ps, [psb], [ostage_b[i]])
                        cx.dma(P[row0 + j * 128:row0 + (j + 1) * 128, c0:c0 + 512], ostage[i][:], [ostage_b[i]], (),
                               ("ost1", i), cwrites=[P_b[(row0 + j * 128) // 512]])
                    return f

                proj(cx, cm, hT3, hT3_b, TT3, Wout, Wout_b,
                     [(c0, 512, "tm", o_consume(c0)) for c0 in range(0, 2048, 512)],
                     [0, 1], nxt=(Wout, Wout_b, 0, 512) if t + 1 < ntile else None)
                if on_tile is not None:
                    on_tile(list(range(row0 // 512, (row0 + TT3) // 512)), is_last and t + 1 == ntile)

        if False and NQB >= 8:
            qsplit = NQB - 2
            attention_pass(0, qsplit)
            phase3(0, qsplit * 512, False)
            attention_pass(qsplit, NQB, mid_hook=mid_hook)
            phase3(qsplit * 512, S, True)
        else:
            attention_pass(0, NQB)
            phase3(0, S, True)


def l1_host_inputs(inp, hh):
    E = 4096
    w_in = inp["f_w_in"][0]
    cs = slice(hh * 2048, (hh + 1) * 2048)
    W1 = np.ascontiguousarray(np.concatenate(
        [w_in[:, 0:E][:, cs], w_in[:, E:2 * E][:, cs], w_in[:, 3 * E:4 * E][:, cs], w_in[:, 2 * E:3 * E][:, cs]], axis=1))
    Wf = np.ascontiguousarray(w_in[:, 4 * E + hh * 16:4 * E + (hh + 1) * 16])
    bfv = np.ascontiguousarray(np.broadcast_to(inp["f_b_forget"][0][hh * 16:(hh + 1) * 16], (128, 16)))
    Wout = np.ascontiguousarray(inp["f_w_out"][0][cs, :])
    nwT = np.ascontiguousarray(inp["norm_w"][1].reshape(16, 128).T)
    return {"nwT": nwT, "W1": W1, "Wf": Wf, "bfv": bfv, "Wout": Wout, "cst": host_consts()}


def l1_scratch(cx, S):
    return {"QT": cx.dram("scrQT", [NH1, 128, S], BF16), "KT": cx.dram("scrKT", [NH1, 128, S], BF16),
            "ZT": cx.dram("scrZT", [NH1, 128, S], BF16), "V": cx.dram("scrV", [S, 2048], BF16),
            "OG": cx.dram("scrOG", [NH1, 128, S], BF16)}


def build_l1(S, TT=2048):
    nc = bass.Bass("TRN2", target_bir_lowering=False)
    with ExitStack() as st:
        cx = Ctx(nc, st)
        x = cx.dram("x", [S, D_MODEL], F32, "ExternalInput")
        cst = cx.dram("cst", [128, 512], F32, "ExternalInput")
        nwT = cx.dram("nwT", [128, KT], F32, "ExternalInput")
        W1 = cx.dram("W1", [D_MODEL, L1_COLS], F32, "ExternalInput")
        Wf = cx.dram("Wf", [D_MODEL, NH1], F32, "ExternalInput")
        bfv = cx.dram("bfv", [128, NH1], F32, "ExternalInput")
        Wout = cx.dram("Wout", [D_MODEL, D_MODEL], F32, "ExternalInput")
        P = cx.dram("P", [S, D_MODEL], F32, "ExternalOutput")
        P_b = Buf("P")
        cm = Common(cx, cst)
        l1_stage(cx, cm, S, x, Buf("x"), nwT, W1, Wf, bfv, Wout, P, P_b, l1_scratch(cx, S), TT=TT)
        cx.p.add("sp", None, [P_b], ())
        cx.p.emit()
    return nc


class Combine:
    def __init__(self, cx, fw_rep=None, q="sp", cm=None, NB=2):
        self.cx = cx
        self.q = q
        self.cm = cm
        self.NB = NB
        self.a = [cx.sb([128, D_MODEL], F32, "ca") for _ in range(NB)]
        self.b = [cx.sb([128, D_MODEL], F32, "cbb") for _ in range(NB)]
        self.c = [cx.sb([128, D_MODEL], F32, "cc") for _ in range(NB)]
        self.a_b = [Buf(f"ca{i}") for i in range(NB)]
        self.b_b = [Buf(f"cb{i}") for i in range(NB)]
        self.c_b = [Buf(f"cc{i}") for i in range(NB)]
        self.final = fw_rep is not None
        if self.final:
            self.fw = cx.sb([128, D_MODEL], F32, "fw"); self.fw_b = Buf("fw")
            cx.dma(self.fw[:], fw_rep, (), [self.fw_b], "fw")
            self.ss = [cx.sb([128, 1], F32, "css") for _ in range(NB)]
            self.ss_b = [Buf(f"css{i}") for i in range(NB)]
            self.rs = [cx.sb([128, 1], F32, "crs") for _ in range(NB)]
            self.rs_b = [Buf(f"crs{i}") for i in range(NB)]
        self.t = 0

    def rows(self, r0, n, xr, xr_b, pa, pa_b, pb, pb_b, out, out_b):
        cx = self.cx
        a, b, c, a_b, b_b, c_b = self.a, self.b, self.c, self.a_b, self.b_b, self.c_b
        for t in range(n // 128):
            i = self.t % self.NB
            self.t += 1
            rows = slice(r0 + t * 128, r0 + (t + 1) * 128)
            cx.dma(a[i][:], xr[rows, :], [xr_b], [a_b[i]], ("ca", i), q=self.q)
            cx.dma(b[i][:], pa[rows, :], [pa_b], [b_b[i]], ("cb", i), q=self.q)
            cx.dma(c[i][:], pb[rows, :], [pb_b], [c_b[i]], ("cc", i), q=self.q)
            cx.tt("dve", b[i][:], b[i][:], c[i][:], ALU.add, [b_b[i], c_b[i]], [b_b[i]])
            cx.tt("dve", a[i][:], a[i][:], b[i][:], ALU.add, [a_b[i], b_b[i]], [a_b[i]])
            if self.final:
                cx.act(c[i][:], a[i][:], AF.Square, [a_b[i]], [c_b[i], self.ss_b[i]],
                       scale=1.0 / math.sqrt(D_MODEL), accum_out=self.ss[i][:])
                if self.cm is not None:
                    sdev(cx, self.rs[i], self.rs_b[i], self.ss[i], self.ss_b[i], self.cm.eps[:], self.cm.eps_b)
                    cx.p.add("dve", (lambda e, o=self.rs[i]: e.reciprocal(o[:], o[:])), [self.rs_b[i]], [self.rs_b[i]])
                    cx.stt("dve", a[i][:], a[i][:], self.rs[i][:, 0:1], self.fw[:], ALU.mult, ALU.mult,
                           [a_b[i], self.rs_b[i], self.fw_b], [a_b[i]])
                else:
                    rstd(cx, self.rs[i], self.rs_b[i], self.ss[i], self.ss_b[i])
                    cx.stt("dve", a[i][:], a[i][:], self.rs[i][:, 0:1], self.fw[:], ALU.mult, ALU.mult,
                           [a_b[i], self.rs_b[i], self.fw_b], [a_b[i]])
            cx.dma(out[rows, :], a[i][:], [a_b[i]], (), ("co", i), cwrites=[out_b], q=self.q)


def combine_stage(cx, R, xr, pa, pb, out, out_b, fw_rep=None, bufs=None):
    xr_b, pa_b, pb_b = bufs if bufs is not None else (Buf("xr"), Buf("pa"), Buf("pb"))
    cb_ = Combine(cx, fw_rep)
    cb_.rows(0, R, xr, xr_b, pa, pa_b, pb, pb_b, out, out_b)


def build_combine(R, final):
    nc = bass.Bass("TRN2", target_bir_lowering=False)
    with ExitStack() as st:
        cx = Ctx(nc, st)
        xr = cx.dram("xr", [R, D_MODEL], F32, "ExternalInput")
        pa = cx.dram("pa", [R, D_MODEL], F32, "ExternalInput")
        pb = cx.dram("pb", [R, D_MODEL], F32, "ExternalInput")
        fw = cx.dram("fw", [128, D_MODEL], F32, "ExternalInput") if final else None
        out = cx.dram("out", [R, D_MODEL], F32, "ExternalOutput")
        out_b = Buf("out")
        combine_stage(cx, R, xr, pa, pb, out, out_b, fw)
        cx.p.add("sp", None, [out_b], ())
        cx.p.emit()
    return nc


PAIR_GROUPS = [[0, 1], [2, 3], [4, 5], [6, 7]]
CC_ROWS = 512


def all_reduce_chunk(cx, src, src_b, dst, dst_b, i, tag):
    rs = slice(i * CC_ROWS, (i + 1) * CC_ROWS)
    cx.p.add("pool", lambda e, rs=rs: e.collective_compute(
        "AllReduce", ALU.add, replica_groups=PAIR_GROUPS, ins=[src[rs, :].opt()], outs=[dst[rs, :].opt()]),
        [src_b], [dst_b], dma_key=(tag, i), inc=1)


def reduce_scatter_chunk(cx, src, src_b, dst, dst_b, i, tag):
    rs = slice(i * CC_ROWS, (i + 1) * CC_ROWS)
    ro = slice(i * (CC_ROWS // 2), (i + 1) * (CC_ROWS // 2))
    cx.p.add("pool", lambda e, rs=rs, ro=ro: e.collective_compute(
        "ReduceScatter", ALU.add, replica_groups=PAIR_GROUPS, ins=[src[rs, :].opt()], outs=[dst[ro, :].opt()]),
        [src_b], [dst_b], dma_key=(tag, i), inc=1)


def build_fused(S):
    nc = bass.Bass("TRN2", target_bir_lowering=False)
    with ExitStack() as st:
        cx = Ctx(nc, st)
        EI = "ExternalInput"
        x = cx.dram("x", [S, D_MODEL], F32, EI)
        cst = cx.dram("cst", [128, 512], F32, EI)
        nwT0 = cx.dram("nwT0", [128, KT], F32, EI)
        W0 = cx.dram("W0", [D_MODEL, L0_COLS], F32, EI)
        Wdt = cx.dram("Wdt", [D_MODEL, 32], F32, EI)
        cwv = cx.dram("cwv", [128, 96], F32, EI)
        cbv = cx.dram("cbv", [128, 24], F32, EI)
        hpv = cx.dram("hpv", [128, 96], F32, EI)
        mnwT = cx.dram("mnwT", [128, KT], F32, EI)
        Wout0 = cx.dram("Wout0", [D_MODEL, D_MODEL], F32, EI)
        nwT1 = cx.dram("nwT1", [128, KT], F32, EI)
        W1 = cx.dram("W1", [D_MODEL, L1_COLS], F32, EI)
        Wf = cx.dram("Wf", [D_MODEL, NH1], F32, EI)
        bfv = cx.dram("bfv", [128, NH1], F32, EI)
        Wout1 = cx.dram("Wout1", [D_MODEL, D_MODEL], F32, EI)
        fw = cx.dram("fw", [128, D_MODEL], F32, EI)
        xh = cx.dram("xh", [S // 2, D_MODEL], F32, EI)
        out = cx.dram("out", [S // 2, D_MODEL], F32, "ExternalOutput")
        R0h = cx.dram("R0h", [S // 2, D_MODEL], F32)
        R1h = cx.dram("R1h", [S // 2, D_MODEL], F32)
        P0 = cx.dram("P0part", [S, D_MODEL], F32)
        P1 = cx.dram("P1part", [S, D_MODEL], F32)
        R0 = cx.dram("R0", [S, D_MODEL], F32)
        R1 = cx.dram("R1", [S, D_MODEL], F32)
        NCK = S // CC_ROWS
        x_b, out_b = Buf("x"), Buf("out")
        P0_b = [Buf(f"P0_{i}") for i in range(NCK)]
        P1_b = [Buf(f"P1_{i}") for i in range(NCK)]
        R0_b = [Buf(f"R0_{i}") for i in range(NCK)]
        R1_b = [Buf(f"R1_{i}") for i in range(NCK)]
        R0h_b = [Buf(f"R0h_{i}") for i in range(NCK)]
        xh_b = Buf("xh")
        cm = Common(cx, cst)
        with cx.scope():
            def on_chunk0(chunks):
                for i in chunks:
                    all_reduce_chunk(cx, P0, P0_b[i], R0, R0_b[i], i, "ar0")
                    reduce_scatter_chunk(cx, P0, P0_b[i], R0h, R0h_b[i], i, "ars0")
            l0_stage(cx, cm, S, x, x_b, nwT0, W0, Wdt, cwv, cbv, hpv, mnwT, Wout0, P0, P0_b, on_chunk=on_chunk0)
        with cx.scope():
            st8 = {"cb": None, "pend": []}

            def p3_setup():
                st8["cb"] = Combine(cx, fw, q="sp", cm=cm, NB=2)

            def flush_pending():
                for i in st8["pend"]:
                    st8["cb"].rows(i * (CC_ROWS // 2), CC_ROWS // 2, xh, xh_b, R0h, R0h_b[i], R1h, R1_b[i], out, out_b)
                st8["pend"] = []

            def on_tile1(chunks, last):
                for i in chunks:
                    reduce_scatter_chunk(cx, P1, P1_b[i], R1h, R1_b[i], i, "ar1")
                st8["q"] = st8.get("q", []) + [list(chunks)]
                if last:
                    for ch in st8["q"]:
                        st8["pend"] = ch
                        flush_pending()
                    st8["q"] = []

            def mid_hook(all_):
                q = st8.get("q", [])
                flat = [i for ch in q for i in ch]
                take = flat if all_ else flat[:1]
                st8["pend"] = take
                flush_pending()
                rest = flat[len(take):]
                st8["q"] = [rest] if rest else []

            l1_stage(cx, cm, S, x, x_b, nwT1, W1, Wf, bfv, Wout1, P1, P1_b, l1_scratch(cx, S), add_ap=R0, add_b=R0_b,
                     on_tile=on_tile1, p3_setup=p3_setup, mid_hook=mid_hook)
        cx.p.add("sp", None, [out_b], ())
        cx.p.emit()
    return nc


N_CORES = 8


def kernel(x, norm_w, m_w_in, m_conv_w, m_conv_b, m_dt_bias, m_A_log, m_D, m_norm_w, m_w_out,
           f_w_in, f_b_forget, f_w_out, final_norm_w):
    inp = dict(x=x, norm_w=norm_w, m_w_in=m_w_in, m_conv_w=m_conv_w, m_conv_b=m_conv_b, m_dt_bias=m_dt_bias,
               m_A_log=m_A_log, m_D=m_D, m_norm_w=m_norm_w, m_w_out=m_w_out, f_w_in=f_w_in,
               f_b_forget=f_b_forget, f_w_out=f_w_out, final_norm_w=final_norm_w)
    inp = {k: np.asarray(v, dtype=np.float32) for k, v in inp.items()}
    B, S, _ = inp["x"].shape
    cores = list(range(N_CORES))
    half = S // 2
    fw = np.ascontiguousarray(np.broadcast_to(inp["final_norm_w"], (128, D_MODEL)))
    per_half = []
    for hh in range(2):
        a = l0_host_inputs(inp, 0, hh)
        b = l1_host_inputs(inp, hh)
        m = {"cst": a["cst"], "nwT0": a["nwT"], "W0": a["W0"], "Wdt": a["Wdt"], "cwv": a["cwv"], "cbv": a["cbv"],
             "hpv": a["hpv"], "mnwT": a["mnwT"], "Wout0": a["Wout"], "nwT1": b["nwT"], "W1": b["W1"], "Wf": b["Wf"],
             "bfv": b["bfv"], "Wout1": b["Wout"], "fw": fw}
        per_half.append(m)
    maps = []
    for c in cores:
        m = dict(per_half[c % 2])
        xb = inp["x"][c // 2]
        m["x"] = np.ascontiguousarray(xb)
        m["xh"] = np.ascontiguousarray(xb.reshape(S // CC_ROWS, 2, CC_ROWS // 2, D_MODEL)[:, c % 2].reshape(half, D_MODEL))
        maps.append(m)
    res = run_bass_kernel_spmd(build_fused(S), maps, core_ids=cores).results
    out = np.empty((B, S, D_MODEL), np.float32)
    for c in cores:
        out[c // 2].reshape(S // CC_ROWS, 2, CC_ROWS // 2, D_MODEL)[:, c % 2] = \
            res[c]["out"].reshape(S // CC_ROWS, CC_ROWS // 2, D_MODEL)
    return out
```

```python
import math
from contextlib import ExitStack

import numpy as np
import concourse.bass as bass
import concourse.mybir as mybir
from concourse.bass_utils import run_bass_kernel_spmd

F32 = mybir.dt.float32
BF16 = mybir.dt.bfloat16
AF = mybir.ActivationFunctionType
ALU = mybir.AluOpType
AX = mybir.AxisListType

D_MODEL = 2048
KT = D_MODEL // 128
EPS = 1e-6
SEM_LIM = 30000


class Buf:
    __slots__ = ("name", "w", "r")

    def __init__(self, name=""):
        self.name = name
        self.w = {}
        self.r = {}


class Op:
    __slots__ = ("eng", "fn", "deps", "sig", "idx", "dma", "key", "val", "pos", "inc")


class Prog:
    ENG = ("pe", "act", "dve", "pool", "sp")

    def __init__(self, nc, stack):
        self.nc = nc
        self.stack = stack
        self.ops = {e: [] for e in self.ENG}
        self.dma_cnt = {}
        self.last_dma = {}

    @staticmethod
    def _k(op):
        return ("d", op.key) if op.dma else ("e", op.eng)

    def add(self, eng, fn, reads=(), writes=(), dma_key=None, inc=16, cwrites=()):
        op = Op()
        op.inc = inc
        op.eng = eng
        op.fn = fn
        op.sig = False
        op.idx = 0
        op.dma = dma_key is not None
        op.key = dma_key
        op.val = 0
        op.pos = len(self.ops[eng])
        deps = {}

        def need(d):
            if d is None or d is op:
                return
            if (not d.dma) and (not op.dma) and d.eng == "pe" and eng == "pe":
                return
            k = self._k(d)
            o = deps.get(k)
            if o is None or (d.val if d.dma else d.pos) > (o.val if o.dma else o.pos):
                deps[k] = d

        for b in reads:
            for d in b.w.values():
                need(d)
        for b in writes:
            for d in b.w.values():
                need(d)
            for d in b.r.values():
                need(d)
        op.deps = list(deps.values())
        for d in op.deps:
            d.sig = True
        if op.dma:
            c = self.dma_cnt.get(dma_key, 0) + inc
            self.dma_cnt[dma_key] = c
            op.val = c
        for b in reads:
            b.r[self._k(op)] = op
        for b in writes:
            b.w = {self._k(op): op}
            b.r = {}
        if op.dma:
            self.last_dma[dma_key] = op
        for b in cwrites:
            b.w[self._k(op)] = op
        self.ops[eng].append(op)
        return op

    def barrier(self):
        lasts = [self.ops[e][-1] for e in self.ENG if self.ops[e] and not self.ops[e][-1].dma]
        lasts = []
        for e in self.ENG:
            for op in reversed(self.ops[e]):
                if not op.dma and op.fn is not None:
                    lasts.append(op)
                    break
        dmas = [d for k, d in self.last_dma.items() if not (isinstance(k, tuple) and str(k[0]).startswith("ar"))]
        for e in self.ENG:
            op = Op()
            op.eng, op.fn, op.sig, op.idx, op.dma, op.key, op.val, op.inc = e, None, False, 0, False, None, 0, 0
            op.pos = len(self.ops[e])
            op.deps = [d for d in lasts if d.eng != e] + dmas
            for d in op.deps:
                d.sig = True
            self.ops[e].append(op)

    def emit(self):
        nc = self.nc
        st = self.stack
        self.esems = {}
        for e in self.ENG:
            c = 0
            for op in self.ops[e]:
                if (not op.dma) and op.sig:
                    c += 1
                    op.idx = c
            n = (c + SEM_LIM - 1) // SEM_LIM
            self.esems[e] = [st.enter_context(nc.semaphore(f"s_{e}{i}")) for i in range(max(n, 1))]
        self.dsems = {k: st.enter_context(nc.semaphore(f"d_{i}")) for i, k in enumerate(self.dma_cnt)}

        def sig_of(d):
            if d.dma:
                return self.dsems[d.key], d.val
            i = d.idx - 1
            return self.esems[d.eng][i // SEM_LIM], (i % SEM_LIM) + 1

        def run(ename, eng):
            waited = {}
            for op in self.ops[ename]:
                for d in op.deps:
                    s, v = sig_of(d)
                    if waited.get(id(s), 0) < v:
                        eng.wait_ge(s, v)
                        waited[id(s)] = v
                if op.fn is None:
                    continue
                ins = op.fn(eng)
                if op.dma:
                    ins.then_inc(self.dsems[op.key], op.inc)
                elif op.sig:
                    s, _ = sig_of(op)
                    ins.then_inc(s, 1)

        with nc.Block() as block:
            @block.tensor
            def _(eng):
                run("pe", eng)

            @block.scalar
            def _(eng):
                run("act", eng)

            @block.vector
            def _(eng):
                run("dve", eng)

            @block.gpsimd
            def _(eng):
                run("pool", eng)

            @block.sync
            def _(eng):
                run("sp", eng)


class Ctx:
    def __init__(self, nc, stack):
        self.nc = nc
        self.st = stack
        self.p = Prog(nc, stack)
        self.n = 0
        self.alloc_st = stack

    def scope(self):
        cx = self

        class _Scope:
            def __enter__(self_):
                self_.prev = cx.alloc_st
                self_.es = ExitStack()
                self_.es.__enter__()
                cx.alloc_st = self_.es
                return self_

            def __exit__(self_, *a):
                cx.p.barrier()
                cx.alloc_st = self_.prev
                return self_.es.__exit__(*a)

        return _Scope()

    def sb(self, shape, dt, name=None):
        self.n += 1
        t = self.alloc_st.enter_context(self.nc.sbuf_tensor(f"{name or 'sb'}_{self.n}", list(shape), dt))
        return t

    def ps(self, name=None):
        self.n += 1
        t = self.st.enter_context(self.nc.psum_tensor(f"{name or 'ps'}_{self.n}", [128, 512], F32))
        return t

    def dram(self, name, shape, dt, kind="Internal"):
        return self.nc.dram_tensor(name, list(shape), dt, kind=kind).ap()

    def dma(self, out, in_, reads, writes, key, q="sp", cwrites=()):
        return self.p.add(q, lambda e: e.dma_start(out=out, in_=in_), reads, writes, dma_key=key, cwrites=cwrites)

    def mm(self, out, lhsT, rhs, start, stop, reads, writes):
        return self.p.add("pe", lambda e: e.matmul(out, lhsT, rhs, start=start, stop=stop), reads, writes)

    def tr(self, out, in_, ident, reads, writes):
        return self.p.add("pe", lambda e: e.transpose(out, in_, ident), reads, writes)

    def act(self, out, in_, func, reads, writes, bias=None, scale=None, accum_out=None, eng="act", cw=()):
        kw = {}
        if bias is not None:
            kw["bias"] = bias
        if scale is not None:
            kw["scale"] = scale
        if accum_out is not None:
            kw["accum_out"] = accum_out
        return self.p.add(eng, lambda e: e.activation(out, in_, func, **kw), reads, writes, cwrites=cw)

    def tt(self, eng, out, in0, in1, op, reads, writes):
        return self.p.add(eng, lambda e: e.tensor_tensor(out, in0, in1, op), reads, writes)

    def ts(self, eng, out, in0, s1, s2, op0, op1, reads, writes):
        if op1 is None:
            return self.p.add(eng, lambda e: e.tensor_scalar(out, in0, s1, None, op0), reads, writes)
        return self.p.add(eng, lambda e: e.tensor_scalar(out, in0, s1, s2, op0, op1), reads, writes)

    def stt(self, eng, out, in0, scalar, in1, op0, op1, reads, writes):
        return self.p.add(eng, lambda e: e.scalar_tensor_tensor(out, in0, scalar, in1, op0, op1), reads, writes)

    def cp(self, eng, out, in_, reads, writes):
        if eng == "act":
            return self.p.add(eng, lambda e: e.copy(out, in_), reads, writes)
        return self.p.add(eng, lambda e: e.tensor_copy(out, in_), reads, writes)

    def memset(self, eng, ap, val, writes):
        return self.p.add(eng, lambda e: e.memset(ap, val), (), writes)


def host_consts():
    i = np.arange(128)
    ident = np.eye(128, dtype=np.float32)
    U = (i[:, None] <= i[None, :]).astype(np.float32)
    SM = (i[:, None] > i[None, :]).astype(np.float32)
    ones = np.ones((128, 128), np.float32)
    return np.ascontiguousarray(np.concatenate([ident, U, SM, ones], axis=1))


class Common:
    def __init__(self, cx, cst_ap):
        self.cx = cx
        nc = cx.nc
        self.cst = cx.sb([128, 512], F32, "cst")
        self.cst_b = Buf("cst")
        cx.dma(self.cst[:], cst_ap, (), [self.cst_b], "cst")
        self.ident_f = self.cst[:, 0:128]
        self.U = self.cst[:, 128:256]
        self.SM = self.cst[:, 256:384]
        self.ones = self.cst[:, 384:512]
        self.cstb = cx.sb([128, 512], BF16, "cstb")
        self.cstb_b = Buf("cstb")
        cx.cp("dve", self.cstb[:], self.cst[:], [self.cst_b], [self.cstb_b])
        self.ident_b = self.cstb[:, 0:128]
        self.U_b = self.cstb[:, 128:256]
        self.ones_b = self.cstb[:, 384:512]
        self.eps = cx.sb([128, 1], F32, "eps")
        self.eps_b = Buf("eps")
        cx.memset("dve", self.eps[:], EPS, [self.eps_b])
        self.banks = [cx.ps(f"bank{i}") for i in range(8)]
        self.bank_b = [Buf(f"bank{i}") for i in range(8)]
        self.wblk = [cx.sb([128, KT, 512], BF16, "wblk") for _ in range(2)]
        self.wblk_b = [[Buf(f"wblk{i}_{q}") for q in range(8)] for i in range(2)]
        self.wblk_i = 0
        self.pending = []

    def prefetch(self, W_ap, W_b, c0, ncols):
        if len(self.pending) >= 1:
            return
        self.pending.append(((id(W_b), c0, ncols), self.load_wblock(W_ap, W_b, c0, ncols)))

    def get(self, W_ap, W_b, c0, ncols):
        if self.pending:
            key, val = self.pending.pop(0)
            assert key == (id(W_b), c0, ncols), "weight prefetch order mismatch"
            return val
        return self.load_wblock(W_ap, W_b, c0, ncols)

    def load_wblock(self, W_ap, W_b, c0, ncols):
        cx = self.cx
        wi = self.wblk_i
        self.wblk_i ^= 1
        wt = self.wblk[wi]
        for q in range(8):
            src = W_ap[q * 256:(q + 1) * 256, c0:c0 + ncols].rearrange("(k p) c -> p k c", p=128)
            cx.dma(wt[:, q * 2:(q + 1) * 2, 0:ncols], src, [W_b], [self.wblk_b[wi][q]], ("w", wi, q), q="pool")
        return wt, self.wblk_b[wi]

    def load_small_w(self, dst, dst_b, W_ap, ncols, key):
        self.cx.dma(dst[:, :, 0:ncols], W_ap.rearrange("(k p) c -> p k c", p=128), (), [dst_b], key, q="pool")


class NormStage:
    def __init__(self, cx, cm, nwT_ap, name, nbuf=2):
        self.cx, self.cm = cx, cm
        self.nbuf = nbuf
        self.nw = cx.sb([128, KT], F32, "nwT")
        self.nw_b = Buf("nwT")
        cx.dma(self.nw[:], nwT_ap, (), [self.nw_b], ("nw", name))
        self.xt = [cx.sb([128, D_MODEL], F32, "xt") for _ in range(nbuf)]
        self.xt_b = [Buf("xt0"), Buf("xt1")]
        self.ss = [cx.sb([128, 1], F32, "ss") for _ in range(2)]
        self.ss_b = [Buf("ss0"), Buf("ss1")]
        self.rs = [cx.sb([128, 1], F32, "rs") for _ in range(2)]
        self.rs_b = [Buf("rs0"), Buf("rs1")]
        self.xn = [cx.sb([128, D_MODEL], BF16, "xn") for _ in range(nbuf)]
        self.xn_b = [Buf("xn0"), Buf("xn1")]
        self.i = 0
        self.xa = None

    def run(self, x_ap, x_b, row0, TT, hT, hT_b, bank, add_ap=None, add_b=None):
        cx, cm = self.cx, self.cm
        ps = cm.banks[bank]
        ps_b = cm.bank_b[bank]
        psv = ps[:].bitcast(BF16)
        for j in range(TT // 128):
            i = self.i
            self.i = (self.i + 1) % self.nbuf
            r0 = row0 + j * 128
            cx.dma(self.xt[i][:], x_ap[r0:r0 + 128, :], [x_b], [self.xt_b[i]], ("xt", i))
            if add_ap is not None:
                if self.xa is None:
                    self.xa = [cx.sb([128, D_MODEL], F32, "xa") for _ in range(2)]
                    self.xa_b = [Buf("xa0"), Buf("xa1")]
                ab = add_b[r0 // 512] if isinstance(add_b, list) else add_b
                cx.dma(self.xa[i][:], add_ap[r0:r0 + 128, :], [ab], [self.xa_b[i]], ("xa", i))
                cx.tt("dve", self.xt[i][:], self.xt[i][:], self.xa[i][:], ALU.add,
                      [self.xt_b[i], self.xa_b[i]], [self.xt_b[i]])
            cx.act(self.xn[i][:], self.xt[i][:], AF.Square, [self.xt_b[i]], [self.xn_b[i], self.ss_b[i]],
                   scale=1.0 / math.sqrt(D_MODEL), accum_out=self.ss[i][:])
            rstd(cx, self.rs[i], self.rs_b[i], self.ss[i], self.ss_b[i])
            cx.ts("dve", self.xn[i][:], self.xt[i][:], self.rs[i][:, 0:1], None, ALU.mult, None,
                  [self.xt_b[i], self.rs_b[i]], [self.xn_b[i]])
            transpose_scale(cx, cm, self.xn[i], self.xn_b[i], hT, hT_b, j * 128, self.nw, self.nw_b, bank)


def rstd(cx, out, out_b, ms, ms_b):
    cx.ts("dve", out[:], ms[:], EPS, None, ALU.add, None, [ms_b], [out_b])
    cx.act(out[:], out[:], AF.Sqrt, [out_b], [out_b])
    cx.p.add("dve", (lambda e, o=out: e.reciprocal(o[:], o[:])), [out_b], [out_b])


def sdev(cx, out, out_b, ms, ms_b, eps_ap, eps_b):
    cx.act(out[:], ms[:], AF.Sqrt, [ms_b, eps_b], [out_b], bias=eps_ap)


def transpose_scale(cx, cm, src, src_b, dstT, dstT_b, c0, wT, wT_b, bank):
    ps = cm.banks[bank]
    ps_b = cm.bank_b[bank]
    psv = ps[:].bitcast(BF16)
    for half in range(2):
        for k in range(8):
            kk = half * 8 + k
            cx.tr(psv[:, k * 128:(k + 1) * 128], src[:, kk * 128:(kk + 1) * 128], cm.ident_b,
                  [src_b, cm.cstb_b], [ps_b])
        cx.tt("dve", dstT[:, half * 8:(half + 1) * 8, c0:c0 + 128],
              psv.rearrange("p (k c) -> p k c", k=8),
              wT[:, half * 8:(half + 1) * 8].unsqueeze(2).to_broadcast([128, 8, 128]), ALU.mult,
              [ps_b, wT_b], [dstT_b])


def proj(cx, cm, hT, hT_b, TT, W_ap, W_b, blocks, banks, nxt=None):
    bi = 0
    for bidx, (c0, ncols, mode, consume) in enumerate(blocks):
        wt, wb = cm.get(W_ap, W_b, c0, ncols)
        if bidx + 1 < len(blocks):
            cm.prefetch(W_ap, W_b, blocks[bidx + 1][0], blocks[bidx + 1][1])
        elif nxt is not None:
            cm.prefetch(*nxt)
        if mode == "cm":
            for ct in range(ncols // 128):
                for t0 in range(0, TT, 512):
                    nt = min(512, TT - t0)
                    bank = banks[bi % len(banks)]
                    bi += 1
                    ps, psb = cm.banks[bank], cm.bank_b[bank]
                    for k in range(KT):
                        cx.mm(ps[:, 0:nt], wt[:, k, ct * 128:(ct + 1) * 128], hT[:, k, t0:t0 + nt],
                              k == 0, k == KT - 1, [wb[k // 2], hT_b], [psb])
                    consume(ps[:, 0:nt], psb, ct, t0, nt)
        else:
            for j in range(TT // 128):
                bank = banks[bi % len(banks)]
                bi += 1
                ps, psb = cm.banks[bank], cm.bank_b[bank]
                for k in range(KT):
                    cx.mm(ps[:, 0:ncols], hT[:, k, j * 128:(j + 1) * 128], wt[:, k, 0:ncols],
                          k == 0, k == KT - 1, [wb[k // 2], hT_b], [psb])
                consume(ps[:, 0:ncols], psb, j)


NH0 = 32
NG0 = 4
L0_COLS = 2048 + 512 + 512 + 2048


def l0_stage(cx, cm, S, x, x_b, nwT, W0, Wdt, cwv, cbv, hpv, mnwT, Wout, P, P_b, TT=512, dbg=None, on_chunk=None):
    if not isinstance(P_b, list):
        P_b = [P_b] * (S // 512)
    NCH = TT // 128
    ns = NormStage(cx, cm, nwT, "l0", nbuf=2)
    W0_b, Wout_b = Buf("W0"), Buf("Wout")
    hT = cx.sb([128, KT, TT], BF16, "hT")
    hT_b = Buf("hT")
    cw = cx.sb([128, 96], F32, "cw"); cw_b = Buf("cw")
    cx.dma(cw[:], cwv, (), [cw_b], "cw")
    cb = cx.sb([128, 24], F32, "cb"); cb_b = Buf("cb")
    cx.dma(cb[:], cbv, (), [cb_b], "cb")
    hp = cx.sb([128, 96], F32, "hp"); hp_b = Buf("hp")
    cx.dma(hp[:], hpv, (), [hp_b], "hp")
    dtb = hp[:, 0:32]
    Drep = hp[:, 64:96]
    Arep = cx.sb([128, 32], F32, "Arep"); A_b = Buf("Arep")
    cx.act(Arep[:], hp[:, 32:64], AF.Exp, [hp_b], [A_b])
    cx.ts("dve", Arep[:], Arep[:], -1.0, None, ALU.mult, None, [A_b], [A_b])
    mnw = cx.sb([128, KT], F32, "mnwT"); mnw_b = Buf("mnwT")
    cx.dma(mnw[:], mnwT, (), [mnw_b], "mnwT")
    wdt = cx.sb([128, KT, 32], BF16, "wdt"); wdt_b = Buf("wdt")
    cm.load_small_w(wdt, wdt_b, Wdt, 32, "wdt")
    pre = [cx.sb([128, 515], F32, "pre") for _ in range(2)]
    pre_b = [Buf("pre0"), Buf("pre1")]
    acc = [cx.sb([128, 512], F32, "acc") for _ in range(2)]
    acc_b = [Buf("acc0"), Buf("acc1")]
    xc = [cx.sb([128, 512], BF16, "xc") for _ in range(2)]
    xc_b = [Buf("xc0"), Buf("xc1")]
    halo = cx.sb([128, 24, 3], F32, "halo"); halo_b = Buf("halo")
    cx.memset("dve", halo[:], 0.0, [halo_b])
    xs_tok = cx.sb([128, NCH, 2048], BF16, "xs_tok"); xs_b = Buf("xs_tok")
    Btok = cx.sb([128, NCH, 512], BF16, "Btok"); Btok_b = Buf("Btok")
    BT = cx.sb([128, NG0, TT], BF16, "BT"); BT_b = Buf("BT")
    CT = cx.sb([128, NG0, TT], BF16, "CT"); CT_b = Buf("CT")
    zs = cx.sb([128, NCH, 2048], BF16, "zs"); zs_b = Buf("zs")
    dt = cx.sb([128, NCH, 32], F32, "dt"); dt_b = Buf("dt")
    dtA = cx.sb([128, NCH, 32], F32, "dtA"); dtA_b = Buf("dtA")
    St = cx.sb([128, 2048], F32, "S"); S_b = [Buf(f"S{g}") for g in range(NG0)]
    Sbf = cx.sb([128, 2048], BF16, "Sbf"); Sbf_b = [Buf(f"Sbf{g}") for g in range(NG0)]
    cx.memset("dve", St[:], 0.0, S_b)
    cx.memset("dve", Sbf[:], 0.0, Sbf_b)
    acs = cx.sb([128, 32], F32, "acs"); acs_b = Buf("acs")
    expacs = [cx.sb([128, 32], F32, "expacs") for _ in range(2)]; expacs_b = [Buf("expacs0"), Buf("expacs1")]
    cd = [cx.sb([128, 32], F32, "cd") for _ in range(2)]; cd_b = [Buf("cd0"), Buf("cd1")]
    dsts = [cx.sb([128, 32], F32, "dsts") for _ in range(2)]; dsts_b = [Buf("dsts0"), Buf("dsts1")]
    rhsm = [cx.sb([128, 4, 128], F32, "rhsm") for _ in range(2)]
    rhsm_b = [Buf("rhsm0"), Buf("rhsm1")]
    LT = [cx.sb([128, 4, 128], F32, "LT") for _ in range(2)]
    LT_b = [Buf("LT0"), Buf("LT1")]
    MT = [cx.sb([128, NH0, 128], BF16, "MT") for _ in range(2)]
    MT_b = [[Buf(f"MT{i}_{g}") for g in range(NG0)] for i in range(2)]
    cbm = cx.sb([128, NG0, 128], F32, "cbm"); cbm_b = Buf("cbm")
    xdt = cx.sb([128, 2048], BF16, "xdt"); xdt_b = Buf("xdt")
    xdtd = cx.sb([128, 2048], BF16, "xdtd"); xdtd_b = Buf("xdtd")
    t1 = [cx.sb([128, 512], F32, "t1") for _ in range(2)]
    t1_b = [Buf("t1_0"), Buf("t1_1")]
    t3 = [cx.sb([128, 512], F32, "t3") for _ in range(2)]
    t3_b = [Buf("t3_0"), Buf("t3_1")]
    gbuf = cx.sb([128, 2048], F32, "gbuf"); gbuf_b = [Buf(f"gbuf{g}") for g in range(NG0)]
    ssq = cx.sb([128, NG0], F32, "ssq"); ssq_b = Buf("ssq")
    rs4 = cx.sb([128, NG0], F32, "rs4"); rs4_b = Buf("rs4")
    ynorm = cx.sb([128, 2048], BF16, "ynorm"); ynorm_b = Buf("ynorm")
    ostage = [cx.sb([128, 512], F32, "ostage") for _ in range(2)]
    ostage_b = [Buf("ost0"), Buf("ost1")]
    cnt = {"conv": 0, "seg": 0, "o": 0}
    B2, B3, B4, B5, B6, B7 = 2, 3, 4, 5, 6, 7

    def bcast_mid(ap2, n):
        return ap2.unsqueeze(2).to_broadcast([128, ap2.shape[1], n])

    def bcast_outer(ap2, m):
        return ap2.unsqueeze(1).to_broadcast([128, m, ap2.shape[1]])

    for t in range(S // TT):
        row0 = t * TT
        ns.run(x, x_b, row0, TT, hT, hT_b, B2)

        deferred = []
        def conv_consume(base_ct, kind):
            def f(ps, psb, ct, t0, nt):
                ctg = base_ct + ct
                i = cnt["conv"] % 2
                cnt["conv"] += 1
                cx.cp("act", pre[i][:, 3:515], ps, [psb], [pre_b[i]])
                cx.cp("dve", pre[i][:, 0:3], halo[:, ctg, :], [halo_b], [pre_b[i]])
                cx.cp("dve", halo[:, ctg, :], pre[i][:, 512:515], [pre_b[i]], [halo_b])
                cx.ts("dve", acc[i][:], pre[i][:, 0:512], cw[:, ctg * 4:ctg * 4 + 1], None, ALU.mult, None,
                      [pre_b[i], cw_b], [acc_b[i]])
                for k in range(1, 4):
                    cx.stt("dve", acc[i][:], pre[i][:, k:k + 512], cw[:, ctg * 4 + k:ctg * 4 + k + 1], acc[i][:],
                           ALU.mult, ALU.add, [pre_b[i], cw_b, acc_b[i]], [acc_b[i]])
                if kind == "x":
                    dst, dst_b = xc[i][:], xc_b[i]
                elif kind == "B":
                    dst, dst_b = BT[:, ct, :], BT_b
                else:
                    dst, dst_b = CT[:, ct, :], CT_b
                cx.act(dst, acc[i][:], AF.Silu, [acc_b[i], cb_b], [dst_b], bias=cb[:, ctg:ctg + 1])
                if kind in ("x", "B"):
                    def do_tr(dst=dst, dst_b=dst_b, ctg=ctg, ct=ct, kind=kind):
                        psv = cm.banks[B2][:].bitcast(BF16)
                        for jj in range(NCH):
                            cx.tr(psv[:, jj * 128:(jj + 1) * 128], dst[:, jj * 128:(jj + 1) * 128], cm.ident_b,
                                  [dst_b, cm.cstb_b], [cm.bank_b[B2]])
                        src = psv[:, 0:NCH * 128].rearrange("p (j c) -> p j c", j=NCH)
                        if kind == "x":
                            cx.cp("dve", xs_tok[:, :, ctg * 128:(ctg + 1) * 128], src, [cm.bank_b[B2]], [xs_b])
                        else:
                            cx.cp("dve", Btok[:, :, ct * 128:(ct + 1) * 128], src, [cm.bank_b[B2]], [Btok_b])
                    while deferred:
                        deferred.pop(0)()
                    deferred.append(do_tr)
            return f

        def z_consume(zb):
            def f(ps, psb, j):
                while deferred:
                    deferred.pop(0)()
                cx.act(zs[:, j, zb * 512:(zb + 1) * 512], ps, AF.Silu, [psb], [zs_b])
            return f

        blocks = []
        for xb_ in range(4):
            blocks.append((xb_ * 512, 512, "cm", conv_consume(xb_ * 4, "x")))
        blocks.append((2048, 512, "cm", conv_consume(16, "B")))
        blocks.append((2560, 512, "cm", conv_consume(20, "C")))
        for zb in range(4):
            blocks.append((3072 + zb * 512, 512, "tm", z_consume(zb)))
        proj(cx, cm, hT, hT_b, TT, W0, W0_b, blocks[:6], [0, 1], nxt=(W0, W0_b, blocks[6][0], 512))
        b3, b3b = cm.banks[B3], cm.bank_b[B3]
        for j in range(NCH):
            for k in range(KT):
                cx.mm(b3[:, j * 32:(j + 1) * 32], hT[:, k, j * 128:(j + 1) * 128], wdt[:, k, :],
                      k == 0, k == KT - 1, [hT_b, wdt_b], [b3b])
        cx.tt("dve", dt[:], b3[:, 0:NCH * 32].rearrange("p (j r) -> p j r", j=NCH),
              bcast_outer(dtb, NCH), ALU.add, [b3b, hp_b], [dt_b])
        cx.act(dt[:], dt[:], AF.Exp, [dt_b], [dt_b])
        cx.act(dt[:], dt[:], AF.Ln, [dt_b], [dt_b], bias=1.0)
        cx.tt("dve", dtA[:], dt[:], bcast_outer(Arep[:], NCH), ALU.mult, [dt_b, A_b], [dtA_b])

        proj(cx, cm, hT, hT_b, TT, W0, W0_b, blocks[6:], [0, 1], nxt=(Wout, Wout_b, 0, 512))

        b3, b3b = cm.banks[B3], cm.bank_b[B3]
        b0, b0b = cm.banks[0], cm.bank_b[0]

        def stage_a(j):
            tok = slice(j * 128, (j + 1) * 128)
            pj = j % 2
            cx.mm(b3[:, 128:160], cm.U, dtA[:, j, :], True, True, [cm.cst_b, dtA_b], [b3b])
            cx.mm(b3[:, 160:192], cm.ones, dtA[:, j, :], True, True, [cm.cst_b, dtA_b], [b3b])
            cx.cp("dve", acs[:], b3[:, 128:160], [b3b], [acs_b])
            cx.act(expacs[pj][:], acs[:], AF.Exp, [acs_b], [expacs_b[pj]])
            cx.act(cd[pj][:], b3[:, 160:192], AF.Exp, [b3b], [cd_b[pj]])
            cx.tt("dve", dsts[pj][:], b3[:, 160:192], acs[:], ALU.subtract, [b3b, acs_b], [dsts_b[pj]])
            cx.act(dsts[pj][:], dsts[pj][:], AF.Exp, [dsts_b[pj]], [dsts_b[pj]])
            for g in range(NG0):
                cx.mm(b0[:, g * 128:(g + 1) * 128], BT[:, g, tok], CT[:, g, tok], True, True, [BT_b, CT_b], [b0b])
            cx.tt("dve", cbm[:], b0[:].rearrange("p (g l) -> p g l", g=NG0), bcast_outer(cm.U, NG0), ALU.mult,
                  [b0b, cm.cst_b], [cbm_b])
            for g in range(NG0):
                for hf in range(2):
                    r0 = g * 8 + hf * 4
                    i = cnt["seg"] % 2
                    cnt["seg"] += 1
                    sbank = (B4, 1)[i]
                    cx.tt("pool", rhsm[i][:], bcast_outer(cm.U, 4), bcast_mid(dtA[:, j, r0:r0 + 4], 128), ALU.mult,
                          [cm.cst_b, dtA_b], [rhsm_b[i]])
                    cx.mm(cm.banks[sbank][:], cm.SM, rhsm[i][:].rearrange("p r l -> p (r l)"), True, True,
                          [cm.cst_b, rhsm_b[i]], [cm.bank_b[sbank]])
                    cx.act(LT[i][:].rearrange("p r l -> p (r l)"), cm.banks[sbank][:], AF.Exp,
                           [cm.bank_b[sbank]], [LT_b[i]])
                    cx.tt("dve", MT[pj][:, r0:r0 + 4, :], LT[i][:], bcast_outer(cbm[:, g, :], 4), ALU.mult,
                          [LT_b[i], cbm_b], [MT_b[pj][g]])

        def stage_b(j):
            tok = slice(j * 128, (j + 1) * 128)
            pj = j % 2
            cx.tt("pool", xdt[:].rearrange("p (r c) -> p r c", r=NH0),
                  xs_tok[:, j, :].rearrange("p (r c) -> p r c", r=NH0), bcast_mid(dt[:, j, :], 64), ALU.mult,
                  [xs_b, dt_b], [xdt_b])
            cx.tt("pool", xdtd[:].rearrange("p (r c) -> p r c", r=NH0),
                  xdt[:].rearrange("p (r c) -> p r c", r=NH0), bcast_mid(dsts[pj][:], 64), ALU.mult,
                  [xdt_b, dsts_b[pj]], [xdtd_b])
            for g in range(NG0):
                gs = slice(g * 512, (g + 1) * 512)
                b5, b5b = cm.banks[B5], cm.bank_b[B5]
                for r in range(8):
                    hh_ = g * 8 + r
                    cx.mm(b5[:, r * 64:(r + 1) * 64], MT[pj][:, hh_, :], xdt[:, hh_ * 64:(hh_ + 1) * 64], True, True,
                          [MT_b[pj][g], xdt_b], [b5b])
                b6, b6b = cm.banks[B6], cm.bank_b[B6]
                cx.mm(b6[:], CT[:, g, tok], Sbf[:, gs], True, True, [CT_b, Sbf_b[g]], [b6b])
                b7, b7b = cm.banks[B7], cm.bank_b[B7]
                cx.mm(b7[:], Btok[:, j, g * 128:(g + 1) * 128], xdtd[:, gs], True, True, [Btok_b, xdtd_b], [b7b])
                ti = g % 2
                cx.tt("dve", t1[ti][:].rearrange("p (r c) -> p r c", r=8), b6[:].rearrange("p (r c) -> p r c", r=8),
                      bcast_mid(expacs[pj][:, g * 8:(g + 1) * 8], 64), ALU.mult, [b6b, expacs_b[pj]], [t1_b[ti]])
                cx.tt("dve", t1[ti][:], t1[ti][:], b5[:], ALU.add, [t1_b[ti], b5b], [t1_b[ti]])
                cx.tt("pool", t3[ti][:].rearrange("p (r c) -> p r c", r=8),
                      xs_tok[:, j, gs].rearrange("p (r c) -> p r c", r=8),
                      bcast_mid(Drep[:, g * 8:(g + 1) * 8], 64), ALU.mult, [xs_b, hp_b], [t3_b[ti]])
                cx.tt("dve", t1[ti][:], t1[ti][:], t3[ti][:], ALU.add, [t1_b[ti], t3_b[ti]], [t1_b[ti]])
                if dbg is not None and "y" in dbg:
                    cx.dma(dbg["y"][row0 + j * 128:row0 + (j + 1) * 128, gs], t1[ti][:], [t1_b[ti]], [dbg["y_b"]],
                           ("dbgy", ti))
                cx.tt("dve", gbuf[:, gs], t1[ti][:], zs[:, j, gs], ALU.mult, [t1_b[ti], zs_b], [gbuf_b[g]])
                cx.act(t3[ti][:], gbuf[:, gs], AF.Square, [gbuf_b[g]], [t3_b[ti], ssq_b],
                       scale=1.0 / math.sqrt(512.0), accum_out=ssq[:, g:g + 1])
                cx.tt("dve", St[:, gs].rearrange("p (r c) -> p r c", r=8), St[:, gs].rearrange("p (r c) -> p r c", r=8),
                      bcast_mid(cd[pj][:, g * 8:(g + 1) * 8], 64), ALU.mult, [S_b[g], cd_b[pj]], [S_b[g]])
                cx.tt("dve", St[:, gs], St[:, gs], b7[:], ALU.add, [S_b[g], b7b], [S_b[g]])
                cx.cp("act", Sbf[:, gs], St[:, gs], [S_b[g]], [Sbf_b[g]])
            rstd(cx, rs4, rs4_b, ssq, ssq_b)
            for g in range(NG0):
                cx.act(ynorm[:, g * 512:(g + 1) * 512], gbuf[:, g * 512:(g + 1) * 512], AF.Copy, [gbuf_b[g], rs4_b],
                       [ynorm_b] if g == 0 else (), scale=rs4[:, g:g + 1], **({} if g == 0 else {"cw": [ynorm_b]}))
            transpose_scale(cx, cm, ynorm, ynorm_b, hT, hT_b, j * 128, mnw, mnw_b, B2)

        stage_a(0)
        for j in range(NCH):
            if j + 1 < NCH:
                stage_a(j + 1)
            stage_b(j)

        def o_consume(c0):
            def f(ps, psb, j):
                i = cnt["o"] % 2
                cnt["o"] += 1
                cx.cp("act", ostage[i][:], ps, [psb], [ostage_b[i]])
                cx.dma(P[row0 + j * 128:row0 + (j + 1) * 128, c0:c0 + 512], ostage[i][:], [ostage_b[i]], (),
                       ("ost", i), cwrites=[P_b[(row0 + j * 128) // 512]])
            return f

        proj(cx, cm, hT, hT_b, TT, Wout, Wout_b, [(c0, 512, "tm", o_consume(c0)) for c0 in range(0, 2048, 512)], [0, 1],
             nxt=(W0, W0_b, 0, 512) if t + 1 < S // TT else None)
        if on_chunk is not None:
            on_chunk(list(range(row0 // 512, (row0 + TT) // 512)))


def l0_host_inputs(inp, b, hh):
    E = 4096
    w_in = inp["m_w_in"][0]
    xs_ = slice(E + hh * 2048, E + (hh + 1) * 2048)
    Bs = slice(2 * E + hh * 512, 2 * E + (hh + 1) * 512)
    Cs = slice(2 * E + 1024 + hh * 512, 2 * E + 1024 + (hh + 1) * 512)
    zs_ = slice(hh * 2048, (hh + 1) * 2048)
    dts = slice(2 * E + 2048 + hh * 32, 2 * E + 2048 + (hh + 1) * 32)
    W0 = np.ascontiguousarray(np.concatenate([w_in[:, xs_], w_in[:, Bs], w_in[:, Cs], w_in[:, zs_]], axis=1))
    Wdt = np.ascontiguousarray(w_in[:, dts])
    cidx = np.concatenate([np.arange(hh * 2048, (hh + 1) * 2048), 4096 + np.arange(hh * 512, (hh + 1) * 512),
                           5120 + np.arange(hh * 512, (hh + 1) * 512)])
    cwv = inp["m_conv_w"][0][:, cidx]
    cwv = np.ascontiguousarray(cwv.reshape(4, 24, 128).transpose(2, 1, 0).reshape(128, 96))
    cbv = np.ascontiguousarray(inp["m_conv_b"][0][cidx].reshape(24, 128).T)
    hs = slice(hh * 32, (hh + 1) * 32)
    hp = np.concatenate([inp["m_dt_bias"][0][hs], inp["m_A_log"][0][hs], inp["m_D"][0][hs]])
    hp = np.ascontiguousarray(np.broadcast_to(hp, (128, 96)))
    mnwT = np.ascontiguousarray(inp["m_norm_w"][0][hh * 2048:(hh + 1) * 2048].reshape(16, 128).T)
    Wout = np.ascontiguousarray(inp["m_w_out"][0][hh * 2048:(hh + 1) * 2048, :])
    nwT = np.ascontiguousarray(inp["norm_w"][0].reshape(16, 128).T)
    return {"nwT": nwT, "W0": W0, "Wdt": Wdt, "cwv": cwv, "cbv": cbv, "hpv": hp, "mnwT": mnwT, "Wout": Wout,
            "cst": host_consts()}


def build_l0(S, TT=512, dbg=False):
    nc = bass.Bass("TRN2", target_bir_lowering=False)
    with ExitStack() as st:
        cx = Ctx(nc, st)
        x = cx.dram("x", [S, D_MODEL], F32, "ExternalInput")
        cst = cx.dram("cst", [128, 512], F32, "ExternalInput")
        nwT = cx.dram("nwT", [128, KT], F32, "ExternalInput")
        W0 = cx.dram("W0", [D_MODEL, L0_COLS], F32, "ExternalInput")
        Wdt = cx.dram("Wdt", [D_MODEL, 32], F32, "ExternalInput")
        cwv = cx.dram("cwv", [128, 96], F32, "ExternalInput")
        cbv = cx.dram("cbv", [128, 24], F32, "ExternalInput")
        hpv = cx.dram("hpv", [128, 96], F32, "ExternalInput")
        mnwT = cx.dram("mnwT", [128, KT], F32, "ExternalInput")
        Wout = cx.dram("Wout", [D_MODEL, D_MODEL], F32, "ExternalInput")
        P = cx.dram("P", [S, D_MODEL], F32, "ExternalOutput")
        P_b = Buf("P")
        d = None
        if dbg:
            d = {"y": cx.dram("dbg_y", [S, 2048], F32, "ExternalOutput"), "y_b": Buf("dbg_y")}
        cm = Common(cx, cst)
        l0_stage(cx, cm, S, x, Buf("x"), nwT, W0, Wdt, cwv, cbv, hpv, mnwT, Wout, P, P_b, TT=TT, dbg=d)
        fin = [P_b] + ([d["y_b"]] if d else [])
        cx.p.add("sp", None, fin, ())
        cx.p.emit()
    return nc


NH1 = 16
L1_COLS = 4 * 2048


def l1_stage(cx, cm, S, x, x_b, nwT, W1, Wf, bfv, Wout, P, P_b, scr, TT=2048, add_ap=None, add_b=None,
             on_tile=None, p3_setup=None, mid_hook=None):
    if not isinstance(P_b, list):
        P_b = [P_b] * (S // 512)
    TT = min(TT, S)
    NT = S // 128
    NQB = S // 512
    QT_, KT_, ZT_, V_, OG_ = scr["QT"], scr["KT"], scr["ZT"], scr["V"], scr["OG"]
    QT_b, KT_b, ZT_b, V_b, OG_b = Buf("QT"), Buf("KT"), Buf("ZT"), Buf("V"), Buf("OG")
    W1_b, Wout_b = Buf("W1"), Buf("Wout1")
    bfr = cx.sb([128, NH1], F32, "bfr"); bfr_b = Buf("bfr")
    cx.dma(bfr[:], bfv, (), [bfr_b], "bfr")
    fl = cx.sb([128, NT, NH1], F32, "fl"); fl_b = Buf("fl")
    cum = cx.sb([128, NT, NH1], F32, "cum"); cum_b = Buf("cum")
    cref = cx.sb([128, NQB, NH1], F32, "cref"); cref_b = Buf("cref")
    carry = [cx.sb([128, NH1], F32, "carry") for _ in range(2)]
    carry_b = [Buf("carry0"), Buf("carry1")]
    cnt = {"st": 0, "o": 0, "pt": 0, "sb": 0}

    with cx.scope():
        ns = NormStage(cx, cm, nwT, "l1")
        hT = cx.sb([128, KT, TT], BF16, "hT1")
        hT_b = Buf("hT1")
        wf = cx.sb([128, KT, NH1], BF16, "wf"); wf_b = Buf("wf")
        cm.load_small_w(wf, wf_b, Wf, NH1, "wf")
        NST = 4
        st = [cx.sb([128, 512], BF16, "st1") for _ in range(NST)]
        st_b = [Buf(f"st1_{i}") for i in range(NST)]
        cm.prefetch(W1, W1_b, 0, 512)
        for t in range(S // TT):
            row0 = t * TT
            ns.run(x, x_b, row0, TT, hT, hT_b, 2, add_ap=add_ap, add_b=add_b)

            def cm_consume(dst, dst_b, blk, silu):
                def f(ps, psb, ct, t0, nt):
                    h = blk * 4 + ct
                    i = cnt["st"] % NST
                    cnt["st"] += 1
                    if silu:
                        cx.act(st[i][:, 0:nt], ps, AF.Silu, [psb], [st_b[i]])
                    else:
                        cx.cp(("act", "dve")[i % 2], st[i][:, 0:nt], ps, [psb], [st_b[i]])
                    cx.dma(dst[h, :, row0 + t0:row0 + t0 + nt], st[i][:, 0:nt], [st_b[i]], (), ("st1", i),
                           cwrites=[dst_b])
                return f

            def v_consume(vb):
                def f(ps, psb, j):
                    i = cnt["st"] % NST
                    cnt["st"] += 1
                    cx.cp(("act", "dve")[i % 2], st[i][:], ps, [psb], [st_b[i]])
                    cx.dma(V_[row0 + j * 128:row0 + (j + 1) * 128, vb * 512:(vb + 1) * 512], st[i][:], [st_b[i]], (),
                           ("st1", i), cwrites=[V_b])
                return f

            blocks = []
            for blk in range(4):
                blocks.append((blk * 512, 512, "cm", cm_consume(QT_, QT_b, blk, False)))
            for blk in range(4):
                blocks.append((2048 + blk * 512, 512, "cm", cm_consume(KT_, KT_b, blk, False)))
            for blk in range(4):
                blocks.append((4096 + blk * 512, 512, "cm", cm_consume(ZT_, ZT_b, blk, True)))
            for blk in range(4):
                blocks.append((6144 + blk * 512, 512, "tm", v_consume(blk)))
            proj(cx, cm, hT, hT_b, TT, W1, W1_b, blocks, [0, 1],
                 nxt=(W1, W1_b, 0, 512) if t + 1 < S // TT else None)
            b3, b3b = cm.banks[3], cm.bank_b[3]
            ntl = TT // 128
            for j in range(ntl):
                for k in range(KT):
                    cx.mm(b3[:, j * NH1:(j + 1) * NH1], hT[:, k, j * 128:(j + 1) * 128], wf[:, k, :],
                          k == 0, k == KT - 1, [hT_b, wf_b], [b3b])
            cx.tt("dve", fl[:, t * ntl:(t + 1) * ntl, :], b3[:, 0:ntl * NH1].rearrange("p (j r) -> p j r", j=ntl),
                  bfr[:].unsqueeze(1).to_broadcast([128, ntl, NH1]), ALU.add, [b3b, bfr_b], [fl_b])

        cx.act(fl[:], fl[:], AF.Exp, [fl_b], [fl_b], scale=-1.0)
        cx.act(fl[:], fl[:], AF.Ln, [fl_b], [fl_b], bias=1.0)
        cx.ts("dve", fl[:], fl[:], -1.0, None, ALU.mult, None, [fl_b], [fl_b])
        cx.memset("dve", carry[0][:], 0.0, [carry_b[0]])
        b3, b3b = cm.banks[3], cm.bank_b[3]
        for i in range(NT):
            c0, c1 = carry[i % 2], carry[(i + 1) % 2]
            c0b, c1b = carry_b[i % 2], carry_b[(i + 1) % 2]
            cx.mm(b3[:, 0:NH1], cm.U, fl[:, i, :], True, True, [cm.cst_b, fl_b], [b3b])
            cx.mm(b3[:, NH1:2 * NH1], cm.ones, fl[:, i, :], True, True, [cm.cst_b, fl_b], [b3b])
            cx.tt("dve", cum[:, i, :], b3[:, 0:NH1], c0[:], ALU.add, [b3b, c0b], [cum_b])
            if i % 4 == 0:
                cx.cp("dve", cref[:, i // 4, :], c0[:], [c0b], [cref_b])
            cx.tt("dve", c1[:], b3[:, NH1:2 * NH1], c0[:], ALU.add, [b3b, c0b], [c1b])

    with cx.scope():
        Kh = [cx.sb([128, S], BF16, "Kh") for _ in range(2)]
        Kh_b = [Buf("Kh0"), Buf("Kh1")]
        Vh = [cx.sb([128, NT, 128], BF16, "Vh") for _ in range(2)]
        Vh_b = [Buf("Vh0"), Buf("Vh1")]
        Qs = [cx.sb([128, 512], BF16, "Qs") for _ in range(3)]
        Qs_b = [Buf(f"Qs{i}") for i in range(3)]
        Zs = [cx.sb([128, 512], BF16, "Zs") for _ in range(3)]
        Zs_b = [Buf(f"Zs{i}") for i in range(3)]
        NPT = 6
        PT = [cx.sb([128, 512], BF16, "PT") for _ in range(NPT)]
        PT_b = [Buf(f"PT{i}") for i in range(NPT)]
        bb = [cx.sb([128, NT], F32, "bb") for _ in range(3)]
        bb_b = [Buf(f"bb{i}") for i in range(3)]
        rden = [cx.sb([128, 512], F32, "rden") for _ in range(2)]
        rden_b = [Buf("rden0"), Buf("rden1")]
        og = [cx.sb([128, 512], F32, "og") for _ in range(2)]
        og_b = [Buf("og0"), Buf("og1")]
        ogz = [cx.sb([128, 512], BF16, "ogz") for _ in range(2)]
        ogz_b = [Buf("ogz0"), Buf("ogz1")]
        TT3 = min(1024, S)
        hT3 = cx.sb([128, KT, TT3], BF16, "hT3")
        hT3_b = Buf("hT3")
        ostage = [cx.sb([128, 512], F32, "ostage1") for _ in range(2)]
        ostage_b = [Buf("ost1_0"), Buf("ost1_1")]
        if p3_setup is not None:
            p3_setup()
        sc = 1.0 / math.sqrt(128.0)
        gctr = {"g": 0, "idx": 0}

        def attention_pass(qlo, qhi, mid_hook=None):
            ntok = qhi * 512
            nq = qhi - qlo

            def load_head(h):
                hi = h % 2
                cx.dma(Kh[hi][:, 0:ntok], KT_[h, :, 0:ntok], [KT_b], [Kh_b[hi]], ("Kh", hi))
                cx.dma(Vh[hi][:, 0:ntok // 128, :],
                       V_[0:ntok, h * 128:(h + 1) * 128].rearrange("(t p) d -> p t d", p=128), [V_b], [Vh_b[hi]],
                       ("Vh", hi))

            g0 = gctr["g"]

            def load_q(gl):
                h, qo = divmod(gl, nq)
                qb = qlo + qo
                i3 = (g0 + gl) % 3
                cx.dma(Qs[i3][:], QT_[h, :, qb * 512:(qb + 1) * 512], [QT_b], [Qs_b[i3]], ("Qs", i3))
                cx.dma(Zs[i3][:], ZT_[h, :, qb * 512:(qb + 1) * 512], [ZT_b], [Zs_b[i3]], ("Zs", i3))
                nk = 4 * qb + 4
                cx.ts("dve", bb[i3][:, 0:nk], cum[:, 0:nk, h], cref[:, qb, h:h + 1], -1.0, ALU.subtract, ALU.mult,
                      [cum_b, cref_b], [bb_b[i3]])

            items = []
            for h in range(NH1):
                for qo in range(nq):
                    for kt in range(4 * (qlo + qo) + 4):
                        items.append((h, qo, kt))
            LA = 2
            load_head(0)
            load_q(0)
            ib = gctr["idx"]

            def issue_qk(li):
                h, qo, kt = items[li]
                qb = qlo + qo
                gl = h * nq + qo
                i3 = (g0 + gl) % 3
                hi = h % 2
                if kt == 0 and gl + 1 < NH1 * nq:
                    load_q(gl + 1)
                if kt == 3 and qo == 0 and h + 1 < NH1:
                    load_head(h + 1)
                if False and mid_hook is not None:
                    mid_hook(h == NH1 - 1)
                m = kt - 4 * qb
                q0 = max(m, 0) * 128
                si = (ib + li) % 3
                sbk, sbb = cm.banks[si], cm.bank_b[si]
                cx.mm(sbk[:, q0:512], Kh[hi][:, kt * 128:(kt + 1) * 128], Qs[i3][:, q0:512], True, True,
                      [Kh_b[hi], Qs_b[i3]], [sbb])

            def issue_rest(li):
                h, qo, kt = items[li]
                qb = qlo + qo
                gl = h * nq + qo
                i2 = (g0 + gl) % 2
                i3 = (g0 + gl) % 3
                hi = h % 2
                nk = 4 * qb + 4
                m = kt - 4 * qb
                q0 = max(m, 0) * 128
                si = (ib + li) % 3
                sbk, sbb = cm.banks[si], cm.bank_b[si]
                pi = (ib + li) % NPT
                ob, obb = cm.banks[4 + i2], cm.bank_b[4 + i2]
                db, dbb = cm.banks[6 + i2], cm.bank_b[6 + i2]
                cx.act(PT[pi][:, q0:512], sbk[:, q0:512], AF.Exp, [sbb, bb_b[i3]], [PT_b[pi]],
                       bias=bb[i3][:, kt:kt + 1], scale=sc)
                if m >= 0:
                    cx.tt("dve", PT[pi][:, q0:q0 + 128], PT[pi][:, q0:q0 + 128], cm.U_b, ALU.mult,
                          [PT_b[pi], cm.cstb_b], [PT_b[pi]])
                cx.mm(ob[:, q0:512], Vh[hi][:, kt, :], PT[pi][:, q0:512], kt == 0, kt == nk - 1,
                      [Vh_b[hi], PT_b[pi]], [obb])
                cx.mm(db[:, q0:512], cm.ones_b, PT[pi][:, q0:512], kt == 0, kt == nk - 1,
                      [cm.cstb_b, PT_b[pi]], [dbb])
                if kt == nk - 1:
                    cx.p.add("dve", (lambda e, o=rden[i2], d=db: e.reciprocal(o[:], d[:])), [dbb], [rden_b[i2]])
                    cx.tt("dve", og[i2][:], ob[:], rden[i2][:], ALU.mult, [obb, rden_b[i2]], [og_b[i2]])
                    cx.tt("dve", ogz[i2][:], og[i2][:], Zs[i3][:], ALU.mult, [og_b[i2], Zs_b[i3]], [ogz_b[i2]])
                    cx.dma(OG_[h, :, qb * 512:(qb + 1) * 512], ogz[i2][:], [ogz_b[i2]], (), ("ogz", i2),
                           cwrites=[OG_b])

            for li in range(len(items) + LA):
                if li < len(items):
                    issue_qk(li)
                if li >= LA:
                    issue_rest(li - LA)
            gctr["g"] += NH1 * nq
            gctr["idx"] += len(items)

        def phase3(row_lo, row_hi, is_last):
            ntile = (row_hi - row_lo) // TT3
            cm.prefetch(Wout, Wout_b, 0, 512)
            for t in range(ntile):
                row0 = row_lo + t * TT3
                cx.dma(hT3[:, :, 0:TT3], OG_[:, :, row0:row0 + TT3].rearrange("c p t -> p c t"), [OG_b], [hT3_b],
                       "hT1ld")

                def o_consume(c0, row0=row0):
                    def f(ps, psb, j):
                        i = cnt["o"] % 2
                        cnt["o"] += 1
                        cx.cp(("act", "dve")[i], ostage[i][:], ps, [psb], [ostage_b[i]])
                        cx.dma(P[row0 + j * 128:row0 + (j + 1) * 128, c0:c0 + 512], ostage[i][:], [ostage_b[i]], (),
                               ("ost1", i), cwrites=[P_b[(row0 + j * 128) // 512]])
                    return f

                proj(cx, cm, hT3, hT3_b, TT3, Wout, Wout_b,
                     [(c0, 512, "tm", o_consume(c0)) for c0 in range(0, 2048, 512)],
                     [0, 1], nxt=(Wout, Wout_b, 0, 512) if t + 1 < ntile else None)
                if on_tile is not None:
                    on_tile(list(range(row0 // 512, (row0 + TT3) // 512)), is_last and t + 1 == ntile)

        if False and NQB >= 8:
            qsplit = NQB - 2
            attention_pass(0, qsplit)
            phase3(0, qsplit * 512, False)
            attention_pass(qsplit, NQB, mid_hook=mid_hook)
            phase3(qsplit * 512, S, True)
        else:
            attention_pass(0, NQB)
            phase3(0, S, True)


def l1_host_inputs(inp, hh):
    E = 4096
    w_in = inp["f_w_in"][0]
    cs = slice(hh * 2048, (hh + 1) * 2048)
    W1 = np.ascontiguousarray(np.concatenate(
        [w_in[:, 0:E][:, cs], w_in[:, E:2 * E][:, cs], w_in[:, 3 * E:4 * E][:, cs], w_in[:, 2 * E:3 * E][:, cs]], axis=1))
    Wf = np.ascontiguousarray(w_in[:, 4 * E + hh * 16:4 * E + (hh + 1) * 16])
    bfv = np.ascontiguousarray(np.broadcast_to(inp["f_b_forget"][0][hh * 16:(hh + 1) * 16], (128, 16)))
    Wout = np.ascontiguousarray(inp["f_w_out"][0][cs, :])
    nwT = np.ascontiguousarray(inp["norm_w"][1].reshape(16, 128).T)
    return {"nwT": nwT, "W1": W1, "Wf": Wf, "bfv": bfv, "Wout": Wout, "cst": host_consts()}


def l1_scratch(cx, S):
    return {"QT": cx.dram("scrQT", [NH1, 128, S], BF16), "KT": cx.dram("scrKT", [NH1, 128, S], BF16),
            "ZT": cx.dram("scrZT", [NH1, 128, S], BF16), "V": cx.dram("scrV", [S, 2048], BF16),
            "OG": cx.dram("scrOG", [NH1, 128, S], BF16)}


def build_l1(S, TT=2048):
    nc = bass.Bass("TRN2", target_bir_lowering=False)
    with ExitStack() as st:
        cx = Ctx(nc, st)
        x = cx.dram("x", [S, D_MODEL], F32, "ExternalInput")
        cst = cx.dram("cst", [128, 512], F32, "ExternalInput")
        nwT = cx.dram("nwT", [128, KT], F32, "ExternalInput")
        W1 = cx.dram("W1", [D_MODEL, L1_COLS], F32, "ExternalInput")
        Wf = cx.dram("Wf", [D_MODEL, NH1], F32, "ExternalInput")
        bfv = cx.dram("bfv", [128, NH1], F32, "ExternalInput")
        Wout = cx.dram("Wout", [D_MODEL, D_MODEL], F32, "ExternalInput")
        P = cx.dram("P", [S, D_MODEL], F32, "ExternalOutput")
        P_b = Buf("P")
        cm = Common(cx, cst)
        l1_stage(cx, cm, S, x, Buf("x"), nwT, W1, Wf, bfv, Wout, P, P_b, l1_scratch(cx, S), TT=TT)
        cx.p.add("sp", None, [P_b], ())
        cx.p.emit()
    return nc


class Combine:
    def __init__(self, cx, fw_rep=None, q="sp", cm=None, NB=2):
        self.cx = cx
        self.q = q
        self.cm = cm
        self.NB = NB
        self.a = [cx.sb([128, D_MODEL], F32, "ca") for _ in range(NB)]
        self.b = [cx.sb([128, D_MODEL], F32, "cbb") for _ in range(NB)]
        self.c = [cx.sb([128, D_MODEL], F32, "cc") for _ in range(NB)]
        self.a_b = [Buf(f"ca{i}") for i in range(NB)]
        self.b_b = [Buf(f"cb{i}") for i in range(NB)]
        self.c_b = [Buf(f"cc{i}") for i in range(NB)]
        self.final = fw_rep is not None
        if self.final:
            self.fw = cx.sb([128, D_MODEL], F32, "fw"); self.fw_b = Buf("fw")
            cx.dma(self.fw[:], fw_rep, (), [self.fw_b], "fw")
            self.ss = [cx.sb([128, 1], F32, "css") for _ in range(NB)]
            self.ss_b = [Buf(f"css{i}") for i in range(NB)]
            self.rs = [cx.sb([128, 1], F32, "crs") for _ in range(NB)]
            self.rs_b = [Buf(f"crs{i}") for i in range(NB)]
        self.t = 0

    def rows(self, r0, n, xr, xr_b, pa, pa_b, pb, pb_b, out, out_b):
        cx = self.cx
        a, b, c, a_b, b_b, c_b = self.a, self.b, self.c, self.a_b, self.b_b, self.c_b
        for t in range(n // 128):
            i = self.t % self.NB
            self.t += 1
            rows = slice(r0 + t * 128, r0 + (t + 1) * 128)
            cx.dma(a[i][:], xr[rows, :], [xr_b], [a_b[i]], ("ca", i), q=self.q)
            cx.dma(b[i][:], pa[rows, :], [pa_b], [b_b[i]], ("cb", i), q=self.q)
            cx.dma(c[i][:], pb[rows, :], [pb_b], [c_b[i]], ("cc", i), q=self.q)
            cx.tt("dve", b[i][:], b[i][:], c[i][:], ALU.add, [b_b[i], c_b[i]], [b_b[i]])
            cx.tt("dve", a[i][:], a[i][:], b[i][:], ALU.add, [a_b[i], b_b[i]], [a_b[i]])
            if self.final:
                cx.act(c[i][:], a[i][:], AF.Square, [a_b[i]], [c_b[i], self.ss_b[i]],
                       scale=1.0 / math.sqrt(D_MODEL), accum_out=self.ss[i][:])
                if self.cm is not None:
                    sdev(cx, self.rs[i], self.rs_b[i], self.ss[i], self.ss_b[i], self.cm.eps[:], self.cm.eps_b)
                    cx.p.add("dve", (lambda e, o=self.rs[i]: e.reciprocal(o[:], o[:])), [self.rs_b[i]], [self.rs_b[i]])
                    cx.stt("dve", a[i][:], a[i][:], self.rs[i][:, 0:1], self.fw[:], ALU.mult, ALU.mult,
                           [a_b[i], self.rs_b[i], self.fw_b], [a_b[i]])
                else:
                    rstd(cx, self.rs[i], self.rs_b[i], self.ss[i], self.ss_b[i])
                    cx.stt("dve", a[i][:], a[i][:], self.rs[i][:, 0:1], self.fw[:], ALU.mult, ALU.mult,
                           [a_b[i], self.rs_b[i], self.fw_b], [a_b[i]])
            cx.dma(out[rows, :], a[i][:], [a_b[i]], (), ("co", i), cwrites=[out_b], q=self.q)


def combine_stage(cx, R, xr, pa, pb, out, out_b, fw_rep=None, bufs=None):
    xr_b, pa_b, pb_b = bufs if bufs is not None else (Buf("xr"), Buf("pa"), Buf("pb"))
    cb_ = Combine(cx, fw_rep)
    cb_.rows(0, R, xr, xr_b, pa, pa_b, pb, pb_b, out, out_b)


def build_combine(R, final):
    nc = bass.Bass("TRN2", target_bir_lowering=False)
    with ExitStack() as st:
        cx = Ctx(nc, st)
        xr = cx.dram("xr", [R, D_MODEL], F32, "ExternalInput")
        pa = cx.dram("pa", [R, D_MODEL], F32, "ExternalInput")
        pb = cx.dram("pb", [R, D_MODEL], F32, "ExternalInput")
        fw = cx.dram("fw", [128, D_MODEL], F32, "ExternalInput") if final else None
        out = cx.dram("out", [R, D_MODEL], F32, "ExternalOutput")
        out_b = Buf("out")
        combine_stage(cx, R, xr, pa, pb, out, out_b, fw)
        cx.p.add("sp", None, [out_b], ())
        cx.p.emit()
    return nc


PAIR_GROUPS = [[0, 1], [2, 3], [4, 5], [6, 7]]
CC_ROWS = 512


def all_reduce_chunk(cx, src, src_b, dst, dst_b, i, tag):
    rs = slice(i * CC_ROWS, (i + 1) * CC_ROWS)
    cx.p.add("pool", lambda e, rs=rs: e.collective_compute(
        "AllReduce", ALU.add, replica_groups=PAIR_GROUPS, ins=[src[rs, :].opt()], outs=[dst[rs, :].opt()]),
        [src_b], [dst_b], dma_key=(tag, i), inc=1)


def reduce_scatter_chunk(cx, src, src_b, dst, dst_b, i, tag):
    rs = slice(i * CC_ROWS, (i + 1) * CC_ROWS)
    ro = slice(i * (CC_ROWS // 2), (i + 1) * (CC_ROWS // 2))
    cx.p.add("pool", lambda e, rs=rs, ro=ro: e.collective_compute(
        "ReduceScatter", ALU.add, replica_groups=PAIR_GROUPS, ins=[src[rs, :].opt()], outs=[dst[ro, :].opt()]),
        [src_b], [dst_b], dma_key=(tag, i), inc=1)


def build_fused(S):
    nc = bass.Bass("TRN2", target_bir_lowering=False)
    with ExitStack() as st:
        cx = Ctx(nc, st)
        EI = "ExternalInput"
        x = cx.dram("x", [S, D_MODEL], F32, EI)
        cst = cx.dram("cst", [128, 512], F32, EI)
        nwT0 = cx.dram("nwT0", [128, KT], F32, EI)
        W0 = cx.dram("W0", [D_MODEL, L0_COLS], F32, EI)
        Wdt = cx.dram("Wdt", [D_MODEL, 32], F32, EI)
        cwv = cx.dram("cwv", [128, 96], F32, EI)
        cbv = cx.dram("cbv", [128, 24], F32, EI)
        hpv = cx.dram("hpv", [128, 96], F32, EI)
        mnwT = cx.dram("mnwT", [128, KT], F32, EI)
        Wout0 = cx.dram("Wout0", [D_MODEL, D_MODEL], F32, EI)
        nwT1 = cx.dram("nwT1", [128, KT], F32, EI)
        W1 = cx.dram("W1", [D_MODEL, L1_COLS], F32, EI)
        Wf = cx.dram("Wf", [D_MODEL, NH1], F32, EI)
        bfv = cx.dram("bfv", [128, NH1], F32, EI)
        Wout1 = cx.dram("Wout1", [D_MODEL, D_MODEL], F32, EI)
        fw = cx.dram("fw", [128, D_MODEL], F32, EI)
        xh = cx.dram("xh", [S // 2, D_MODEL], F32, EI)
        out = cx.dram("out", [S // 2, D_MODEL], F32, "ExternalOutput")
        R0h = cx.dram("R0h", [S // 2, D_MODEL], F32)
        R1h = cx.dram("R1h", [S // 2, D_MODEL], F32)
        P0 = cx.dram("P0part", [S, D_MODEL], F32)
        P1 = cx.dram("P1part", [S, D_MODEL], F32)
        R0 = cx.dram("R0", [S, D_MODEL], F32)
        R1 = cx.dram("R1", [S, D_MODEL], F32)
        NCK = S // CC_ROWS
        x_b, out_b = Buf("x"), Buf("out")
        P0_b = [Buf(f"P0_{i}") for i in range(NCK)]
        P1_b = [Buf(f"P1_{i}") for i in range(NCK)]
        R0_b = [Buf(f"R0_{i}") for i in range(NCK)]
        R1_b = [Buf(f"R1_{i}") for i in range(NCK)]
        R0h_b = [Buf(f"R0h_{i}") for i in range(NCK)]
        xh_b = Buf("xh")
        cm = Common(cx, cst)
        with cx.scope():
            def on_chunk0(chunks):
                for i in chunks:
                    all_reduce_chunk(cx, P0, P0_b[i], R0, R0_b[i], i, "ar0")
            l0_stage(cx, cm, S, x, x_b, nwT0, W0, Wdt, cwv, cbv, hpv, mnwT, Wout0, P0, P0_b, on_chunk=on_chunk0)
        with cx.scope():
            st8 = {"cb": None, "pend": []}

            for i in range(NCK):
                reduce_scatter_chunk(cx, P0, P0_b[i], R0h, R0h_b[i], i, "ars0")

            def p3_setup():
                st8["cb"] = Combine(cx, fw, q="sp", cm=cm, NB=2)

            def flush_pending():
                for i in st8["pend"]:
                    st8["cb"].rows(i * (CC_ROWS // 2), CC_ROWS // 2, xh, xh_b, R0h, R0h_b[i], R1h, R1_b[i], out, out_b)
                st8["pend"] = []

            def on_tile1(chunks, last):
                for i in chunks:
                    reduce_scatter_chunk(cx, P1, P1_b[i], R1h, R1_b[i], i, "ar1")
                st8["q"] = st8.get("q", []) + [list(chunks)]
                if last:
                    for ch in st8["q"]:
                        st8["pend"] = ch
                        flush_pending()
                    st8["q"] = []

            def mid_hook(all_):
                q = st8.get("q", [])
                flat = [i for ch in q for i in ch]
                take = flat if all_ else flat[:1]
                st8["pend"] = take
                flush_pending()
                rest = flat[len(take):]
                st8["q"] = [rest] if rest else []

            l1_stage(cx, cm, S, x, x_b, nwT1, W1, Wf, bfv, Wout1, P1, P1_b, l1_scratch(cx, S), add_ap=R0, add_b=R0_b,
                     on_tile=on_tile1, p3_setup=p3_setup, mid_hook=mid_hook)
        cx.p.add("sp", None, [out_b], ())
        cx.p.emit()
    return nc


N_CORES = 8


def kernel(x, norm_w, m_w_in, m_conv_w, m_conv_b, m_dt_bias, m_A_log, m_D, m_norm_w, m_w_out,
           f_w_in, f_b_forget, f_w_out, final_norm_w):
    inp = dict(x=x, norm_w=norm_w, m_w_in=m_w_in, m_conv_w=m_conv_w, m_conv_b=m_conv_b, m_dt_bias=m_dt_bias,
               m_A_log=m_A_log, m_D=m_D, m_norm_w=m_norm_w, m_w_out=m_w_out, f_w_in=f_w_in,
               f_b_forget=f_b_forget, f_w_out=f_w_out, final_norm_w=final_norm_w)
    inp = {k: np.asarray(v, dtype=np.float32) for k, v in inp.items()}
    B, S, _ = inp["x"].shape
    cores = list(range(N_CORES))
    half = S // 2
    fw = np.ascontiguousarray(np.broadcast_to(inp["final_norm_w"], (128, D_MODEL)))
    per_half = []
    for hh in range(2):
        a = l0_host_inputs(inp, 0, hh)
        b = l1_host_inputs(inp, hh)
        m = {"cst": a["cst"], "nwT0": a["nwT"], "W0": a["W0"], "Wdt": a["Wdt"], "cwv": a["cwv"], "cbv": a["cbv"],
             "hpv": a["hpv"], "mnwT": a["mnwT"], "Wout0": a["Wout"], "nwT1": b["nwT"], "W1": b["W1"], "Wf": b["Wf"],
             "bfv": b["bfv"], "Wout1": b["Wout"], "fw": fw}
        per_half.append(m)
    maps = []
    for c in cores:
        m = dict(per_half[c % 2])
        xb = inp["x"][c // 2]
        m["x"] = np.ascontiguousarray(xb)
        m["xh"] = np.ascontiguousarray(xb.reshape(S // CC_ROWS, 2, CC_ROWS // 2, D_MODEL)[:, c % 2].reshape(half, D_MODEL))
        maps.append(m)
    res = run_bass_kernel_spmd(build_fused(S), maps, core_ids=cores).results
    out = np.empty((B, S, D_MODEL), np.float32)
    for c in cores:
        out[c // 2].reshape(S // CC_ROWS, 2, CC_ROWS // 2, D_MODEL)[:, c % 2] = \
            res[c]["out"].reshape(S // CC_ROWS, CC_ROWS // 2, D_MODEL)
    return out
```
